# Optimizing a Trainium2 kernel written in Bass

```python
import math
import jax, jax.numpy as jnp
from jax import lax
import numpy as np

D_MODEL = 1024
BATCH = 1
SEQ = 16384
DEPTH = 1

GRID_W = 64
HEAD_DIM = 64
N_HEADS_DIL = 8
N_HEADS_NA = 8
WIDTH_DIL = N_HEADS_DIL * HEAD_DIM
WIDTH_NA = N_HEADS_NA * HEAD_DIM
DIL_PATTERNS = ((128, 1), (512, 4), (2048, 16))
QUERY_BLOCK = 64
NA_ROWS_MAX = 8
NA_COLS = 16
T5_BUCKETS = 32
T5_MAX_DISTANCE = 1024
D_FF = 2816
CONV_WIDTH = 3
RMS_EPS = 1e-6
IN_SPLITS = (WIDTH_DIL, WIDTH_DIL, WIDTH_DIL, WIDTH_NA, WIDTH_NA, WIDTH_NA, D_MODEL, D_MODEL)
IN_COLS = sum(IN_SPLITS)

kernel_name = "hybrid_dilated_neighbourhood_gated_encoder"


def rmsnorm(x, g):
    x32 = x.astype(jnp.float32)
    y = x32 * lax.rsqrt(jnp.mean(x32 * x32, axis=-1, keepdims=True) + RMS_EPS)
    return (y * g.astype(jnp.float32)).astype(x.dtype)


def t5_bucket(rel):
    n = T5_BUCKETS // 2
    max_exact = n // 2
    sign_part = jnp.where(rel > 0, n, 0)
    a = jnp.abs(rel)
    af = jnp.maximum(a, 1).astype(jnp.float32)
    large = max_exact + (jnp.log(af / max_exact) / math.log(T5_MAX_DISTANCE / max_exact)
                         * (n - max_exact)).astype(jnp.int32)
    large = jnp.minimum(large, n - 1)
    return sign_part + jnp.where(a < max_exact, a, large)


def dilated_window_attention(q, k, v, rel_bias, window, dilation):
    B, S, H, Dh = q.shape
    L = S // dilation
    R = window // (2 * dilation)
    nb = -(-L // QUERY_BLOCK)
    Lp = nb * QUERY_BLOCK
    KB = QUERY_BLOCK + 2 * R

    def to_res(t):
        return t.reshape(B, L, dilation, H, Dh).transpose(0, 2, 1, 3, 4)

    qr, kr, vr = to_res(q), to_res(k), to_res(v)
    qb = jnp.pad(qr, ((0, 0), (0, 0), (0, Lp - L), (0, 0), (0, 0))).reshape(B, dilation, nb, QUERY_BLOCK, H, Dh)
    pad_kv = ((0, 0), (0, 0), (R, Lp - L + R), (0, 0), (0, 0))
    key_idx = jnp.arange(nb)[:, None] * QUERY_BLOCK + jnp.arange(KB)[None, :]
    kb = jnp.pad(kr, pad_kv)[:, :, key_idx]
    vb = jnp.pad(vr, pad_kv)[:, :, key_idx]

    delta = jnp.arange(KB)[None, :] - jnp.arange(QUERY_BLOCK)[:, None] - R
    key_pos = key_idx - R
    mask = ((key_pos >= 0) & (key_pos < L))[:, None, :] & (jnp.abs(delta) <= R)[None]
    bias = rel_bias[t5_bucket(delta * dilation)].astype(jnp.float32).transpose(2, 0, 1)

    scale = HEAD_DIM ** -0.5
    s = jnp.einsum('bgnqhd,bgnkhd->bghnqk', qb, kb).astype(jnp.float32) * scale + bias[:, None]
    s = jnp.where(mask, s, -jnp.inf)
    m = jnp.max(s, axis=-1, keepdims=True)
    p = jnp.exp(s - m)
    l = jnp.sum(p, axis=-1, keepdims=True)
    o = jnp.einsum('bghnqk,bgnkhd->bgnqhd', (p / l).astype(v.dtype), vb)
    lse = (m + jnp.log(l))[..., 0]

    o = o.reshape(B, dilation, Lp, H, Dh)[:, :, :L].transpose(0, 2, 1, 3, 4).reshape(B, S, H, Dh)
    lse = lse.transpose(0, 1, 3, 4, 2).reshape(B, dilation, Lp, H)[:, :, :L]
    lse = lse.transpose(0, 2, 1, 3).reshape(B, S, H)
    return o, lse


def mixture_of_dilations(q, k, v, rel_bias):
    outs, lses = [], []
    for window, dilation in DIL_PATTERNS:
        o, lse = dilated_window_attention(q, k, v, rel_bias, window, dilation)
        outs.append(o)
        lses.append(lse)
    w = jax.nn.softmax(jnp.stack(lses, axis=0), axis=0)
    return jnp.einsum('gbsh,gbshd->bshd', w.astype(q.dtype), jnp.stack(outs, axis=0))


def neighbourhood_attention_2d(q, k, v, rpb):
    B, S, H, Dh = q.shape
    rows = S // GRID_W
    kh = min(NA_ROWS_MAX, rows)
    q5 = q.reshape(B, rows, GRID_W, H, Dh)
    k5 = k.reshape(B, rows, GRID_W, H, Dh)
    v5 = v.reshape(B, rows, GRID_W, H, Dh)

    r = jnp.arange(rows)
    row_start = jnp.clip(r - kh // 2, 0, rows - kh)
    row_idx = row_start[:, None] + jnp.arange(kh)[None, :]
    kr = k5[:, row_idx]
    vr = v5[:, row_idx]

    cq = jnp.arange(GRID_W)
    col_start = jnp.clip(cq - NA_COLS // 2, 0, GRID_W - NA_COLS)
    col_mask = (cq[None, :] >= col_start[:, None]) & (cq[None, :] < col_start[:, None] + NA_COLS)

    dr = row_idx - r[:, None]
    dc = jnp.clip(cq[None, :] - cq[:, None], -(NA_COLS - 1), NA_COLS - 1)
    bias = rpb.astype(jnp.float32)[:, (dr + NA_ROWS_MAX - 1)[:, :, None, None], (dc + NA_COLS - 1)[None, None]]
    bias = bias.transpose(0, 1, 3, 2, 4)

    scale = HEAD_DIM ** -0.5
    s = jnp.einsum('brqhd,brakhd->bhrqak', q5, kr).astype(jnp.float32) * scale + bias[None]
    s = jnp.where(col_mask[:, None, :], s, -jnp.inf)
    p = jax.nn.softmax(s, axis=(-2, -1))
    o = jnp.einsum('bhrqak,brakhd->brqhd', p.astype(v.dtype), vr)
    return o.reshape(B, S, H, Dh)


def depthwise_conv_seq(u, w, b):
    C = u.shape[-1]
    pad = CONV_WIDTH // 2
    y = lax.conv_general_dilated(u, w.astype(u.dtype)[:, None, :], window_strides=(1,),
                                 padding=((pad, pad),), dimension_numbers=('NWC', 'WIO', 'NWC'),
                                 feature_group_count=C)
    return y + b.astype(u.dtype)


def setup_inputs(seed: int = 0) -> dict:
    key = jax.random.key(seed)
    ks = jax.random.split(key, 20)
    f32 = jnp.float32
    nrm = lambda k, shape, s: (jax.random.normal(k, shape, f32) * s)
    return {
        "x": nrm(ks[0], (BATCH, SEQ, D_MODEL), 1.0),
        "c": nrm(ks[1], (BATCH, D_MODEL), 1.0),
        "w_ada": nrm(ks[2], (DEPTH, D_MODEL, 6 * D_MODEL), D_MODEL ** -0.5),
        "b_ada": nrm(ks[3], (DEPTH, 6 * D_MODEL), 0.02),
        "g_mix": 1.0 + nrm(ks[4], (DEPTH, D_MODEL), 0.02),
        "w_in": nrm(ks[5], (DEPTH, D_MODEL, IN_COLS), D_MODEL ** -0.5),
        "rel_bias": nrm(ks[6], (T5_BUCKETS, N_HEADS_DIL), 0.5),
        "na_rpb": nrm(ks[7], (DEPTH, N_HEADS_NA, 2 * NA_ROWS_MAX - 1, 2 * NA_COLS - 1), 0.5),
        "w_branch_dil": nrm(ks[8], (DEPTH, WIDTH_DIL, D_MODEL), WIDTH_DIL ** -0.5),
        "w_branch_na": nrm(ks[9], (DEPTH, WIDTH_NA, D_MODEL), WIDTH_NA ** -0.5),
        "w_out": nrm(ks[10], (DEPTH, D_MODEL, D_MODEL), D_MODEL ** -0.5),
        "g_ffn": 1.0 + nrm(ks[11], (DEPTH, D_MODEL), 0.02),
        "w_up": nrm(ks[12], (DEPTH, D_MODEL, 2 * D_FF), D_MODEL ** -0.5),
        "conv_w": nrm(ks[13], (DEPTH, CONV_WIDTH, 2 * D_FF), CONV_WIDTH ** -0.5),
        "conv_b": nrm(ks[14], (DEPTH, 2 * D_FF), 0.01),
        "w_down": nrm(ks[15], (DEPTH, D_FF, D_MODEL), D_FF ** -0.5),
        "g_final": 1.0 + nrm(ks[16], (D_MODEL,), 0.02),
    }


def reference(x, c, w_ada, b_ada, g_mix, w_in, rel_bias, na_rpb, w_branch_dil, w_branch_na,
              w_out, g_ffn, w_up, conv_w, conv_b, w_down, g_final):
    B, S, D = x.shape
    split_at = list(np.cumsum(IN_SPLITS)[:-1])
    for i in range(DEPTH):
        ada = jax.nn.silu(c) @ w_ada[i] + b_ada[i]
        shift1, scale1, gate1, shift2, scale2, gate2 = [t[:, None, :] for t in jnp.split(ada, 6, axis=-1)]

        h = rmsnorm(x, g_mix[i]) * (1 + scale1) + shift1
        proj = h @ w_in[i]
        qa, ka, va, qb, kb, vb, ga, gb = jnp.split(proj, split_at, axis=-1)
        heads = lambda t, n: t.reshape(B, S, n, HEAD_DIM)
        oa = mixture_of_dilations(heads(qa, N_HEADS_DIL), heads(ka, N_HEADS_DIL), heads(va, N_HEADS_DIL), rel_bias)
        ob = neighbourhood_attention_2d(heads(qb, N_HEADS_NA), heads(kb, N_HEADS_NA), heads(vb, N_HEADS_NA), na_rpb[i])
        ya = oa.reshape(B, S, WIDTH_DIL) @ w_branch_dil[i]
        yb = ob.reshape(B, S, WIDTH_NA) @ w_branch_na[i]
        merged = jax.nn.sigmoid(ga) * ya + jax.nn.sigmoid(gb) * yb
        x = x + gate1 * (merged @ w_out[i])

        h2 = rmsnorm(x, g_ffn[i]) * (1 + scale2) + shift2
        u = depthwise_conv_seq(h2 @ w_up[i], conv_w[i], conv_b[i])
        u_val, u_gate = jnp.split(u, 2, axis=-1)
        x = x + gate2 * ((jax.nn.gelu(u_gate, approximate=True) * u_val) @ w_down[i])
    return rmsnorm(x, g_final)
```

```python
import math
from contextlib import ExitStack

import numpy as np
import ml_dtypes

import concourse.bass as bass
import concourse.mybir as mybir
from concourse.bass_utils import run_bass_kernel_spmd

F32 = mybir.dt.float32
BF16 = mybir.dt.bfloat16
AF = mybir.ActivationFunctionType
ALU = mybir.AluOpType

S_TOT = 16384
NCORE = 8
TOK = 2048
SL = 4224
SOFF = 1088
QOFF = 1024
NQ = 2176
BKOFF = 704
NBK = 2816
D_FF = 2816
NEG = -30000.0
DIL = (1, 4, 16)
FW = 344
GRAN = 256

ENGS = ("sync", "gpsimd", "scalar", "vector", "tensor")


class Op:
    __slots__ = ("eng", "fn", "dma", "semkey", "ndma", "deps", "sigval", "need_sig", "idx")


class Sched:
    def __init__(self):
        self.ops = {e: [] for e in ENGS}
        self.last_w = {}
        self.readers = {}
        self.dma_last = {}
        self.dma_count = {}
        self.all_ops = []

    def add(self, eng, fn, reads=(), writes=(), dma=False, semkey=None, ndma=1):
        op = Op()
        op.eng, op.fn, op.dma, op.semkey, op.ndma = eng, fn, dma, semkey, ndma
        op.need_sig = dma
        op.sigval = None
        deps = {}
        for k in reads:
            w = self.last_w.get(k)
            if w is not None:
                deps[id(w)] = (w, True)
            if isinstance(k, tuple) and k[0] == "ps":
                for r in self.readers.get(k, ()):
                    if r.eng != eng and id(r) not in deps:
                        deps[id(r)] = (r, True)
        for k in writes:
            w = self.last_w.get(k)
            if w is not None and id(w) not in deps:
                deps[id(w)] = (w, False)
            for r in self.readers.get(k, ()):
                if id(r) not in deps:
                    deps[id(r)] = (r, False)
        if dma:
            prev = self.dma_last.get(semkey)
            if prev is not None:
                deps[id(prev)] = (prev, True)
            self.dma_last[semkey] = op
            self.dma_count[semkey] = self.dma_count.get(semkey, 0) + ndma
            op.sigval = 16 * self.dma_count[semkey]
        op.deps = []
        for w, raw in deps.values():
            if w is op:
                continue
            if (not w.dma) and (not dma) and w.eng == eng and not raw and eng == "tensor":
                continue
            op.deps.append(w)
            w.need_sig = True
        for k in reads:
            self.readers.setdefault(k, []).append(op)
        for k in writes:
            self.last_w[k] = op
            self.readers[k] = []
        op.idx = len(self.all_ops)
        self.all_ops.append(op)
        self.ops[eng].append(op)
        return op

    def emit(self, nc, st):
        eng_sem = {e: st.enter_context(nc.semaphore("se_" + e)) for e in ENGS}
        dma_sem = {k: st.enter_context(nc.semaphore("sd_%d" % i)) for i, k in enumerate(self.dma_count)}
        for e in ENGS:
            n = 0
            for op in self.ops[e]:
                if not op.dma and op.need_sig:
                    n += 1
                    op.sigval = n
        block = st.enter_context(nc.Block())

        def run(ename, e):
            waited = {}
            for op in self.ops[ename]:
                for w in op.deps:
                    sem = dma_sem[w.semkey] if w.dma else eng_sem[w.eng]
                    if waited.get(sem.num, 0) >= w.sigval:
                        continue
                    waited[sem.num] = w.sigval
                    e.wait_ge(sem, w.sigval)
                r = op.fn(e)
                if op.dma:
                    rs = r if isinstance(r, (list, tuple)) else [r]
                    assert len(rs) == op.ndma
                    for ins in rs:
                        ins.then_inc(dma_sem[op.semkey], 16)
                elif op.need_sig:
                    r.then_inc(eng_sem[ename], 1)
            if ename == "sync":
                for k, cnt in self.dma_count.items():
                    e.wait_ge(dma_sem[k], 16 * cnt)

        block.sync(lambda e: run("sync", e))
        block.gpsimd(lambda e: run("gpsimd", e))
        block.scalar(lambda e: run("scalar", e))
        block.vector(lambda e: run("vector", e))
        block.tensor(lambda e: run("tensor", e))


class Buf:
    def __init__(self, arena_name, arena_ap, off, shape, dtype):
        self.an, self.off, self.shape, self.dtype = arena_name, off, tuple(shape), dtype
        self.esz = 4 if dtype == F32 else 2
        n = 1
        for s in shape[1:]:
            n *= s
        self.n = n
        assert off % 4 == 0
        a = arena_ap[:, off // 2: off // 2 + n * self.esz // 2]
        if dtype == F32:
            a = a.bitcast(F32)
        if len(shape) == 3:
            a = a.rearrange("p (a b) -> p a b", a=shape[1])
        elif len(shape) == 4:
            a = a.rearrange("p (a b c) -> p a b c", a=shape[1], b=shape[2])
        self.ap = a

    def k(self, lo=0, hi=None):
        if hi is None:
            hi = self.n
        b0 = (self.off + lo * self.esz) // GRAN
        b1 = (self.off + hi * self.esz - 1) // GRAN
        return [(self.an, g) for g in range(b0, b1 + 1)]

    def k2(self, c, lo, hi):
        inner = self.shape[2] if len(self.shape) == 3 else self.shape[2] * self.shape[3]
        return self.k(c * inner + lo, c * inner + hi)


def _t5_bucket(rel):
    rel = np.asarray(rel, dtype=np.int64)
    n = 16
    max_exact = 8
    sign_part = np.where(rel > 0, n, 0)
    a = np.abs(rel)
    af = np.maximum(a, 1).astype(np.float32)
    large = max_exact + (np.log(af / np.float32(max_exact)) / np.float32(math.log(1024 / max_exact))
                         * np.float32(n - max_exact)).astype(np.int32)
    large = np.minimum(large, n - 1)
    return sign_part + np.where(a < max_exact, a, large)


def _a_geom():
    out = []
    for d in DIL:
        nqc = NQ // d
        jq0 = QOFF // d
        span = nqc + 128
        nt = (span + 127) // 128
        tl = [min(128, span - 128 * t) for t in range(nt)]
        blocks = []
        b = 0
        while 256 * b < nqc:
            blocks.append((b, min(256, nqc - 256 * b)))
            b += 1
        out.append((d, jq0, nqc, nt, blocks, tl))
    return out


A_GEOM = _a_geom()


def _a_tile_index():
    idx = {}
    n = 0
    for g, (d, jq0, nqc, nt, blocks, tl) in enumerate(A_GEOM):
        for r in range(d):
            for t in range(nt):
                idx[(g, r, t)] = n
                n += 1
    return idx, n


A_TIDX, NKT = _a_tile_index()


def _b_groups():
    gs = []
    for G in range(9):
        nr = min(4, 34 - 4 * G)
        lo, hi = 2 * G, min(2 * G + 5, 20)
        if G == 0:
            hi = 6
        if G == 8:
            lo = 15
        gs.append((G, nr, list(range(lo, hi + 1))))
    return gs


B_GROUPS = _b_groups()


def _b_unit_index():
    idx = {}
    n = 0
    for G, nr, tiles in B_GROUPS:
        for lm in tiles:
            idx[(G, lm)] = n
            n += 4
    return idx, n


B_UIDX, NVB = _b_unit_index()


def build_program():
    nc = bass.Bass("TRN2", target_bir_lowering=False)
    din = lambda name, shape, dt=F32: nc.dram_tensor(name, list(shape), dt, kind="ExternalInput").ap()
    xT_d = din("xT", [1024, SL])
    cT_d = din("cT", [128, 8])
    wada_d = din("wada", [1024, 6144])
    bada_d = din("badaT", [128, 48])
    gT_d = din("gT", [128, 24])
    win_d = din("w_in", [1024, 5120])
    wbd_d = din("w_bd", [512, 1024])
    wbn_d = din("w_bn", [512, 1024])
    wout_d = din("w_out", [1024, 1024])
    wup_d = din("w_up", [1024, 5632])
    conv_d = din("convT", [128, 4, 44])
    wdn_d = din("w_down", [D_FF, 1024])
    ga_d = din("GA", [128, 3, 8, 256])
    tb_d = din("TBt", [128, 8, 1024])
    kma_d = din("kmaskA", [128, NKT])
    valb_d = din("valB", [128, NVB], BF16)
    indb_d = din("indB", [128, 128], BF16)
    ident_d = din("ident", [128, 128], BF16)
    zm_d = din("zmask", [128, 2])
    yT_d = nc.dram_tensor("yT", [1024, TOK], F32, kind="ExternalOutput").ap()

    S = Sched()
    with ExitStack() as st:
        sb = lambda name, shape, dt: st.enter_context(nc.sbuf_tensor(name, list(shape), dt))
        BIG_T = sb("BIG", [128, 69632 // 2], BF16)
        AR_T = sb("ARENA", [128, 90112 // 2], BF16)
        A2_T = sb("A2", [128, 49152 // 2], BF16)
        CT = sb("CT", [128, 8], F32)
        SCT = sb("SCT", [128, 8], BF16)
        BADA = sb("BADA", [128, 48], F32)
        GTC = sb("GTC", [128, 24], F32)
        CONV = sb("CONV", [128, 4, 44], F32)
        KMA = sb("KMA", [128, NKT], F32)
        VALB = sb("VALB", [128, NVB], BF16)
        INDB = sb("INDB", [128, 128], BF16)
        IDENT = sb("IDENT", [128, 128], BF16)
        ZM = sb("ZM", [128, 2], F32)
        ADAT = sb("ADAT", [128, 48], F32)
        COL = sb("COL", [128, 16], F32)
        TMPC = sb("TMPC", [128, 16], F32)
        ONES = sb("ONES", [128, 128], BF16)
        ONESA = sb("ONESA", [128, 128], BF16)
        ONESB = sb("ONESB", [128, 128], BF16)
        ONE11 = sb("ONE11", [1, 2], F32)
        PS_T = st.enter_context(nc.psum_tensor("PS", [128, 8, 512], F32))
        PS = [PS_T[:, i, :] for i in range(8)]
        psk = lambda i: [("ps", i)]

        BIG = lambda off, shape, dt: Buf("BIG", BIG_T, off, shape, dt)
        AR = lambda off, shape, dt: Buf("AR", AR_T, off, shape, dt)
        A2 = lambda off, shape, dt: Buf("A2", A2_T, off, shape, dt)

        hT = BIG(0, [128, 8, SL], BF16)
        xmid = BIG(0, [128, 8, NQ], F32)
        OAB = AR(0, [128, 8, NQ], BF16)
        kT = AR(34816, [128, SL], BF16)
        qT = AR(43264, [128, NQ], BF16)
        vT = AR(47616, [128, SL], BF16)
        VB = AR(56064, [128, 24, 2, 128], BF16)
        UACC = AR(68352, [128, NQ], F32)
        DACC = AR(77056, [128, NQ], F32)
        MERG = AR(34816, [128, 8, NQ], BF16)
        WUP = AR(0, [128, 44, 8, 128], BF16)
        ADAROW = AR(32768, [128, 6144], F32)

        cq = [0]

        def cdma(out_ap, in_ap, wkeys):
            k = "c%d" % (cq[0] % 4)
            cq[0] += 1
            S.add("sync", lambda e: e.dma_start(out=out_ap, in_=in_ap), writes=wkeys, dma=True, semkey=k)

        cdma(CT[:], cT_d, ["CT"])
        cdma(BADA[:], bada_d, ["BADA"])
        cdma(GTC[:], gT_d, ["GTC"])
        cdma(CONV[:], conv_d, ["CONV"])
        cdma(KMA[:], kma_d, ["KMA"])
        cdma(VALB[:], valb_d, ["VALB"])
        cdma(INDB[:], indb_d, ["INDB"])
        cdma(IDENT[:], ident_d, ["IDENT"])
        cdma(ZM[:], zm_d, ["ZM"])
        S.add("gpsimd", lambda e: e.memset(ONES[:], 1.0), writes=["ONES"])
        S.add("gpsimd", lambda e: e.memset(ONESA[:, 64:128], 0.0), writes=["ONESA0"])
        S.add("gpsimd", lambda e: e.memset(ONESA[:, 0:64], 1.0), writes=["ONESA"])
        S.add("gpsimd", lambda e: e.memset(ONESB[:, 0:64], 0.0), writes=["ONESB0"])
        S.add("gpsimd", lambda e: e.memset(ONESB[:, 64:128], 1.0), writes=["ONESB"])
        S.add("gpsimd", lambda e: e.memset(ONE11[:], 1.0), writes=["ONE11"])
        S.add("scalar", lambda e: e.activation(out=SCT[:], in_=CT[:], func=AF.Silu), reads=["CT"], writes=["SCT"])

        wada_v = wada_d.rearrange("(c p) f -> p c f", p=128)
        WADA = [AR(57344, [128, 8, 1024], BF16), AR(73728, [128, 8, 1024], BF16)]
        for i in range(2):
            wb = WADA[i]
            S.add("gpsimd", (lambda e, wb=wb, i=i: e.dma_start(out=wb.ap, in_=wada_v[:, :, 1024 * i:1024 * (i + 1)])),
                  writes=wb.k(), dma=True, semkey="wada%d" % (i % 2))
            for jj in range(8):
                j = 8 * i + jj
                for kc in range(8):
                    S.add("tensor", (lambda e, wb=wb, kc=kc, jj=jj, j=j: e.matmul(
                        PS[0][:, j:j + 1], lhsT=wb.ap[:, kc, 128 * jj:128 * jj + 128], rhs=SCT[:, kc:kc + 1],
                        start=(kc == 0), stop=(kc == 7))),
                        reads=["SCT"] + wb.k2(kc, 128 * jj, 128 * jj + 128), writes=psk(0))
        S.add("vector", lambda e: e.tensor_tensor(out=ADAT[:, 0:16], in0=PS[0][:, 0:16], in1=BADA[:, 0:16], op=ALU.add),
              reads=psk(0) + ["BADA"], writes=["ADAT"])
        S.add("vector", lambda e: e.tensor_scalar(out=TMPC[:, 0:8], in0=ADAT[:, 8:16], scalar1=1.0, scalar2=None,
                                                  op0=ALU.add), reads=["ADAT"], writes=["TMPC"])
        S.add("vector", lambda e: e.tensor_tensor(out=COL[:, 0:8], in0=TMPC[:, 0:8], in1=GTC[:, 0:8], op=ALU.mult),
              reads=["TMPC", "GTC"], writes=["COL"])
        WADH = AR(26112, [128, 8, 512], BF16)

        def ada_def_load(p):
            S.add("gpsimd", lambda e: e.dma_start(out=WADH.ap, in_=wada_v[:, :, 2048 + 512 * p:2048 + 512 * (p + 1)]),
                  writes=WADH.k(), dma=True, semkey="wadh")

        def ada_def_mm(p):
            for jj in range(4):
                for kc in range(8):
                    S.add("tensor", (lambda e, kc=kc, jj=jj: e.matmul(
                        PS[7][:, jj:jj + 1], lhsT=WADH.ap[:, kc, 128 * jj:128 * jj + 128], rhs=SCT[:, kc:kc + 1],
                        start=(kc == 0), stop=(kc == 7))),
                        reads=["SCT"] + WADH.k2(kc, 128 * jj, 128 * jj + 128), writes=psk(7))
            c0 = 16 + 4 * p
            S.add("vector", lambda e: e.tensor_tensor(out=ADAT[:, c0:c0 + 4], in0=PS[7][:, 0:4], in1=BADA[:, c0:c0 + 4],
                                                      op=ALU.add), reads=psk(7) + ["BADA"], writes=["ADATb"])
            if p == 7:
                S.add("vector", lambda e: e.tensor_scalar(out=TMPC[:, 8:16], in0=ADAT[:, 32:40], scalar1=1.0, scalar2=None,
                                                          op0=ALU.add), reads=["ADATb"], writes=["TMPC2"])
                S.add("vector", lambda e: e.tensor_tensor(out=COL[:, 8:16], in0=TMPC[:, 8:16], in1=GTC[:, 8:16],
                                                          op=ALU.mult), reads=["TMPC2", "GTC"], writes=["COL2"])
        a1 = lambda c: COL[:, c:c + 1]
        b1 = lambda c: ADAT[:, c:c + 1]
        gate1 = lambda c: ADAT[:, 16 + c:17 + c]
        b2 = lambda c: ADAT[:, 24 + c:25 + c]
        a2 = lambda c: COL[:, 8 + c:9 + c]
        gate2 = lambda c: ADAT[:, 40 + c:41 + c]
        gfin = lambda c: GTC[:, 16 + c:17 + c]

        VBF = AR(56064, [128, 6144], BF16)
        slab_tiles = [(512 * i, 512) for i in range(8)] + [(4096, 128)]
        win_v = win_d.rearrange("(c p) f -> p c f", p=128)
        WCH = [A2(0, [128, 8, 384], BF16), A2(6144, [128, 8, 384], BF16)]
        TAB = [A2(12288, [128, 2048], F32), A2(20480, [128, 2048], F32)]
        TT = [A2(28672 + 2048 * i, [128, 2, 256], F32) for i in range(6)]
        PP = [A2(40960 + 1024 * i, [128, 2, 256], BF16) for i in range(6)]
        pending = []
        DEPTH = 5

        def flush():
            while pending:
                pending.pop(0)()
        evac_rr = [0]
        unit_rr = [0]
        blk_rr = [0]

        def evac(out_ap, in_ap, rkeys, wkeys, scale=None):
            evac_rr[0] += 1
            if evac_rr[0] % 2 == 0:
                if scale is None:
                    S.add("scalar", lambda e: e.activation(out=out_ap, in_=in_ap, func=AF.Copy), reads=rkeys, writes=wkeys)
                else:
                    S.add("scalar", lambda e: e.activation(out=out_ap, in_=in_ap, func=AF.Copy, scale=scale),
                          reads=rkeys, writes=wkeys)
            else:
                if scale is None:
                    S.add("vector", lambda e: e.tensor_copy(out=out_ap, in_=in_ap), reads=rkeys, writes=wkeys)
                else:
                    S.add("vector", lambda e: e.tensor_scalar(out=out_ap, in0=in_ap, scalar1=scale, scalar2=None,
                                                              op0=ALU.mult), reads=rkeys, writes=wkeys)

        proj_rr = [0]

        def project(wch, wcol, dst, dst_off_fn, tiles, hoff, scale=None):
            for (t0, W) in tiles:
                bank = 2 + proj_rr[0] % 6
                proj_rr[0] += 1
                for kc in range(8):
                    S.add("tensor", (lambda e, kc=kc, t0=t0, W=W, bank=bank: e.matmul(
                        PS[bank][:, 0:W], lhsT=wch.ap[:, kc, wcol:wcol + 128], rhs=hT.ap[:, kc, hoff + t0:hoff + t0 + W],
                        start=(kc == 0), stop=(kc == 7))),
                        reads=wch.k2(kc, 0, 384) + hT.k2(kc, hoff + t0, hoff + t0 + W), writes=psk(bank))
                d0 = dst_off_fn(t0)
                evac(dst.ap[:, d0:d0 + W], PS[bank][:, 0:W], psk(bank), dst.k(d0, d0 + W), scale)

        def unit(KL, n, kap_fn, qap_fn, tab_ap, kmask_ap, val_fn, vslot, U_ap, D_ap, udbank, first, last, rk_extra,
                 after=()):
            u = unit_rr[0] % 3
            u4 = unit_rr[0] % 6
            unit_rr[0] += 1
            sb0, sb1 = 2 + 2 * u, 3 + 2 * u
            tt, pp = TT[u4], PP[u4]
            for hh in range(2):
                bank = sb0 + hh
                has_val = val_fn is not None
                S.add("tensor", (lambda e, hh=hh, bank=bank, has_val=has_val: e.matmul(
                    PS[bank][0:KL, 0:n], lhsT=kap_fn(hh), rhs=qap_fn(hh), start=True, stop=not has_val)),
                    reads=kT.k() + qT.k(), writes=psk(bank))
                if has_val:
                    S.add("tensor", (lambda e, hh=hh, bank=bank: e.matmul(
                        PS[bank][0:KL, 0:n], lhsT=INDB[64 * hh:64 * hh + 64, :], rhs=val_fn(hh), start=False, stop=True)),
                        reads=["INDB", "VALB"], writes=psk(bank))
            S.add("vector", lambda e: e.tensor_tensor(out=tt.ap[0:KL, :, 0:n], in0=PS_T[0:KL, sb0:sb0 + 2, 0:n],
                                                      in1=tab_ap, op=ALU.add),
                  reads=psk(sb0) + psk(sb1) + rk_extra, writes=tt.k())
            if kmask_ap is not None:
                S.add("scalar", lambda e: e.activation(out=pp.ap[0:KL, :, 0:n], in_=tt.ap[0:KL, :, 0:n], func=AF.Exp,
                                                       bias=kmask_ap), reads=tt.k() + ["KMA"], writes=pp.k())
            else:
                S.add("scalar", lambda e: e.activation(out=pp.ap[0:KL, :, 0:n], in_=tt.ap[0:KL, :, 0:n], func=AF.Exp),
                      reads=tt.k(), writes=pp.k())

            def back():
                seq = [(U_ap[0:64], VB.ap[0:KL, vslot, 0, 0:64], 0), (U_ap[64:128], VB.ap[0:KL, vslot, 1, 64:128], 1),
                       (D_ap[0:64], ONES[0:KL, 0:64], 0), (D_ap[64:128], ONES[0:KL, 0:64], 1)]
                for i, (o_ap, l_ap, hh) in enumerate(seq):
                    st_ = first and i < 2
                    sp_ = last and i == 3
                    S.add("tensor", (lambda e, o_ap=o_ap, l_ap=l_ap, hh=hh, st_=st_, sp_=sp_: e.matmul(
                        o_ap, lhsT=l_ap, rhs=pp.ap[0:KL, hh, 0:n], start=st_, stop=sp_, skip_group_check=True)),
                        reads=pp.k() + VB.k(256 * vslot, 256 * (vslot + 1)) + ["ONES"], writes=psk(udbank))
                for f in after:
                    f()
            pending.append(back)
            while len(pending) > DEPTH:
                pending.pop(0)()

        def vtrans(src_ap_fn, ntile, kl_fn):
            flush()
            for g0 in range(0, ntile, 4):
                bank = 2 + proj_rr[0] % 6
                proj_rr[0] += 1
                cnt = min(4, ntile - g0)
                for i in range(cnt):
                    KL = kl_fn(g0 + i)
                    S.add("tensor", (lambda e, i=i, g0=g0, KL=KL, bank=bank: e.matmul(
                        PS[bank][0:KL, 128 * i:128 * i + 128], lhsT=src_ap_fn(g0 + i, KL), rhs=IDENT[:],
                        start=True, stop=True)), reads=vT.k() + ["IDENT"], writes=psk(bank))
                src = PS[bank][:, 0:128 * cnt].rearrange("p (t h d) -> p t h d", t=cnt, h=2)
                dstv = VB.ap[:, g0:g0 + cnt, :, :]
                for hh in range(2):
                    evac(dstv[:, :, hh, 64 * hh:64 * hh + 64], src[:, :, hh, :], psk(bank), VB.k(256 * g0, 256 * (g0 + cnt)))

        qtiles = [(512 * i, 512) for i in range(4)] + [(2048, 128)]
        ga_v = ga_d
        wdn_s = nc.dram_tensor("wdn_scratch", [8, 128, 22 * 128], BF16).ap()
        wdn_pre_v = wdn_d.rearrange("(f p) m -> p f m", p=128)

        def load_pair(p):
            S.add("gpsimd", lambda e: e.dma_start(out=wdn_s[p].rearrange("p (f m) -> p f m", f=22),
                                                  in_=wdn_pre_v[:, :, 128 * p:128 * p + 128]),
                  writes=[("wdn_s", p)], dma=True, semkey="wdpre%d" % (p % 2))
            wch, tab, j = WCH[p % 2], TAB[p % 2], p % 4
            cbase = 0 if p < 4 else 1536

            def wl(e):
                return [e.dma_start(out=wch.ap[:, :, 128 * i:128 * i + 128],
                                    in_=win_v[:, :, cbase + 512 * i + 128 * j:cbase + 512 * i + 128 * j + 128])
                        for i in range(3)]
            S.add("gpsimd", wl, writes=wch.k(), dma=True, semkey="wch%d" % (p % 2), ndma=3)
            if p < 4:
                tabA_ = tab.ap[:, 0:1536].rearrange("p (g h c) -> p g h c", g=3, h=2)
                S.add("sync", lambda e: e.dma_start(out=tabA_, in_=ga_v[:, :, 2 * j:2 * j + 2, :]),
                      writes=tab.k(), dma=True, semkey="tab%d" % (p % 2))
            else:
                tabB_ = tab.ap.rearrange("p (h c) -> p h c", h=2)
                S.add("sync", lambda e: e.dma_start(out=tabB_, in_=tb_d[:, 2 * j:2 * j + 2, :]),
                      writes=tab.k(), dma=True, semkey="tab%d" % (p % 2))

        xT_v = xT_d.rearrange("(c p) s -> p c s", p=128)
        XT = [AR(0, [128, 8, 512], F32), AR(16384, [128, 8, 512], F32), A2(20480, [128, 8, 512], F32)]
        SQH = [AR(56064, [128, 8, 512], BF16), AR(64256, [128, 8, 512], BF16)]
        RSH = [AR(72448, [128, 512], F32), AR(74496, [128, 512], F32), AR(76544, [128, 512], F32)]
        TMPH = [AR(78592, [128, 512], F32), AR(80640, [128, 512], F32)]
        load_pair(0)

        def h_front(ti):
            s0, W = slab_tiles[ti]
            xt, sqb, rs, bank = XT[ti % 3], SQH[ti % 2], RSH[ti % 3], ti % 2
            S.add("sync", lambda e: e.dma_start(out=xt.ap[:, :, 0:W], in_=xT_v[:, :, s0:s0 + W]),
                  writes=xt.k(), dma=True, semkey="xt%d" % (ti % 3))
            S.add("scalar", lambda e: e.activation(out=sqb.ap[:, 0:4, 0:W], in_=xt.ap[:, 0:4, 0:W], func=AF.Square),
                  reads=xt.k(0, 4 * 512), writes=sqb.k(0, 4 * 512))
            S.add("gpsimd", lambda e: e.tensor_tensor(out=sqb.ap[:, 4:8, 0:W], in0=xt.ap[:, 4:8, 0:W],
                                                      in1=xt.ap[:, 4:8, 0:W], op=ALU.mult),
                  reads=xt.k(4 * 512, 8 * 512), writes=sqb.k(4 * 512, 8 * 512))
            for c in range(8):
                S.add("tensor", (lambda e, c=c: e.matmul(PS[bank][:, 0:W], lhsT=ONES[:], rhs=sqb.ap[:, c, 0:W],
                                                         start=(c == 0), stop=(c == 7))),
                      reads=["ONES"] + sqb.k2(c, 0, 512), writes=psk(bank))

        def h_front_b(ti):
            s0, W = slab_tiles[ti]
            rs, bank = RSH[ti % 3], ti % 2
            S.add("scalar", lambda e: e.activation(out=rs.ap[:, 0:W], in_=PS[bank][:, 0:W], func=AF.Sqrt,
                                                   bias=1e-6, scale=1.0 / 1024.0),
                  reads=psk(bank), writes=rs.k())
            S.add("vector", lambda e: e.reciprocal(out=rs.ap[:, 0:W], in_=rs.ap[:, 0:W]), reads=rs.k(), writes=rs.k())

        def h_back(ti):
            s0, W = slab_tiles[ti]
            xt, rs = XT[ti % 3], RSH[ti % 3]
            for c in range(8):
                tm = TMPH[c % 2]
                S.add("vector", (lambda e, c=c, tm=tm: e.scalar_tensor_tensor(
                    out=tm.ap[:, 0:W], in0=xt.ap[:, c, 0:W], scalar=a1(c), in1=rs.ap[:, 0:W],
                    op0=ALU.mult, op1=ALU.mult)),
                    reads=xt.k2(c, 0, 512) + rs.k() + ["COL"], writes=tm.k())
                S.add("scalar", (lambda e, c=c, tm=tm: e.activation(
                    out=hT.ap[:, c, s0:s0 + W], in_=tm.ap[:, 0:W], func=AF.Identity, bias=b1(c))),
                    reads=tm.k() + ["ADAT"], writes=hT.k2(c, s0, s0 + W))

        def h_proj(ti):
            s0, W = slab_tiles[ti]
            project(WCH[0], 128, kT, lambda t0: t0, [(s0, W)], 0)
            project(WCH[0], 256, vT, lambda t0: t0, [(s0, W)], 0)
            if QOFF <= s0 < QOFF + NQ:
                project(WCH[0], 0, qT, lambda t0: t0, [(s0 - QOFF, min(W, QOFF + NQ - s0))], QOFF, scale=0.125)

        h_front(0)
        h_front_b(0)
        for ti in range(len(slab_tiles)):
            if ti + 1 < len(slab_tiles):
                h_front(ti + 1)
            h_back(ti)
            if ti + 1 < len(slab_tiles):
                h_front_b(ti + 1)
            h_proj(ti)
        S.add("gpsimd", lambda e: e.memset(VBF.ap, 0.0), writes=VB.k())

        for j in range(4):
            wch = WCH[j % 2]
            tab = TAB[j % 2]
            tabA = tab.ap[:, 0:1536].rearrange("p (g h c) -> p g h c", g=3, h=2)
            ada_def_load(2 * j)
            if j > 0:
                project(wch, 128, kT, lambda t0: t0, slab_tiles, 0)
                project(wch, 256, vT, lambda t0: t0, slab_tiles, 0)
                project(wch, 0, qT, lambda t0: t0, qtiles, QOFF, scale=0.125)
            load_pair(j + 1)
            ada_def_mm(2 * j)
            ada_def_load(2 * j + 1)
            for g, (d, jq0, nqc, nt, blocks, tl) in enumerate(A_GEOM):
                if g == 1:
                    ada_def_mm(2 * j + 1)
                rgroups = [list(range(d))] if d * nt <= 24 else [list(range(0, 8)), list(range(8, 16))]
                for rg in rgroups:
                    slot_of = {}
                    lst = []
                    for r in rg:
                        for t in range(nt):
                            slot_of[(r, t)] = len(lst)
                            lst.append((r, t))

                    def vsrc(i, KL, lst=lst, d=d, jq0=jq0):
                        r, t = lst[i]
                        s0 = r + d * (jq0 - 64 + 128 * t)
                        return vT.ap[:, s0:s0 + d * (KL - 1) + 1:d]
                    vtrans(vsrc, len(lst), lambda i, lst=lst, tl=tl: tl[lst[i][1]])
                    for r in rg:
                        for (b, nqb) in blocks:
                            bu = blk_rr[0] % 2
                            blk_rr[0] += 1
                            tiles = [t for t in range(nt) if (128 * t - 64 + tl[t] + 64 > 256 * b) and
                                     (128 * t - 64 - 64 < 256 * b + nqb)]
                            c_lo = r + d * 256 * b
                            c_hi = c_lo + d * (nqb - 1) + 1
                            uv = UACC.ap[:, c_lo:c_hi:d]
                            dv = DACC.ap[:, c_lo:c_hi:d]

                            def evac_blk(g=g, uv=uv, dv=dv, bu=bu, nqb=nqb, c_lo=c_lo, c_hi=c_hi):
                                if g == 0:
                                    S.add("scalar", lambda e: e.activation(out=uv, in_=PS[bu][:, 0:nqb], func=AF.Copy),
                                          reads=psk(bu), writes=UACC.k(c_lo, c_hi))
                                    S.add("scalar", lambda e: e.activation(out=dv, in_=PS[bu][:, 256:256 + nqb], func=AF.Copy),
                                          reads=psk(bu), writes=DACC.k(c_lo, c_hi))
                                else:
                                    S.add("vector", lambda e: e.tensor_tensor(out=uv, in0=PS[bu][:, 0:nqb], in1=uv, op=ALU.add),
                                          reads=psk(bu) + UACC.k(c_lo, c_hi), writes=UACC.k(c_lo, c_hi))
                                    S.add("vector", lambda e: e.tensor_tensor(out=dv, in0=PS[bu][:, 256:256 + nqb], in1=dv,
                                                                              op=ALU.add),
                                          reads=psk(bu) + DACC.k(c_lo, c_hi), writes=DACC.k(c_lo, c_hi))
                            for ti_, t in enumerate(tiles):
                                KL = tl[t]
                                kA = 128 * t - 64
                                n0 = max(0, kA - 64 - 256 * b)
                                n1 = min(nqb, kA + KL + 64 - 256 * b)
                                n = n1 - n0
                                c0 = 256 * b + n0 - kA + 64
                                ks0 = r + d * (jq0 + kA)
                                qs0 = r + d * (256 * b + n0)
                                kap = lambda hh, ks0=ks0, KL=KL, d=d: kT.ap[64 * hh:64 * hh + 64, ks0:ks0 + d * (KL - 1) + 1:d]
                                qap = lambda hh, qs0=qs0, n=n, d=d: qT.ap[64 * hh:64 * hh + 64, qs0:qs0 + d * (n - 1) + 1:d]
                                tab_ap = tabA[0:KL, g, :, c0:c0 + n]
                                kidx = A_TIDX[(g, r, t)]
                                lastt = ti_ == len(tiles) - 1
                                unit(KL, n, kap, qap, tab_ap, KMA[0:KL, kidx:kidx + 1], None, slot_of[(r, t)],
                                     PS[bu][:, n0:n1], PS[bu][:, 256 + n0:256 + n1], bu,
                                     ti_ == 0, lastt, tab.k(), after=([evac_blk] if lastt else ()))
            flush()
            S.add("vector", lambda e: e.tensor_scalar(out=DACC.ap, in0=DACC.ap, scalar1=1e-30, scalar2=None, op0=ALU.add),
                  reads=DACC.k(), writes=DACC.k())
            S.add("vector", lambda e: e.reciprocal(out=DACC.ap, in_=DACC.ap), reads=DACC.k(), writes=DACC.k())
            S.add("gpsimd", (lambda e, j=j: e.tensor_tensor(out=OAB.ap[:, j, :], in0=UACC.ap, in1=DACC.ap, op=ALU.mult)),
                  reads=UACC.k() + DACC.k(), writes=OAB.k2(j, 0, NQ))

        WM = [A2(0, [128, 24, 128], BF16), A2(6144, [128, 24, 128], BF16)]
        wbd_v = wbd_d.rearrange("(c p) f -> p c f", p=128)
        wbn_v = wbn_d.rearrange("(c p) f -> p c f", p=128)

        def wloadm_j(j):
            wm = WM[j % 2]

            def wl(e):
                return [e.dma_start(out=wm.ap[:, 0:8, :], in_=win_v[:, :, 3072 + 128 * j:3072 + 128 * j + 128]),
                        e.dma_start(out=wm.ap[:, 8:16, :], in_=win_v[:, :, 4096 + 128 * j:4096 + 128 * j + 128]),
                        e.dma_start(out=wm.ap[:, 16:20, :], in_=wbd_v[:, :, 128 * j:128 * j + 128]),
                        e.dma_start(out=wm.ap[:, 20:24, :], in_=wbn_v[:, :, 128 * j:128 * j + 128])]
            S.add("gpsimd", wl, writes=wm.k(), dma=True, semkey="wm%d" % (j % 2), ndma=4)

        bk_tiles = [(512 * i, 512) for i in range(5)] + [(2560, 256)]
        for j in range(4):
            wch = WCH[j % 2]
            tab = TAB[j % 2]

            tabB = tab.ap.rearrange("p (h c) -> p h c", h=2)
            project(wch, 128, kT, lambda t0: t0, bk_tiles, BKOFF)
            project(wch, 256, vT, lambda t0: t0, bk_tiles, BKOFF)
            project(wch, 0, qT, lambda t0: t0, qtiles, QOFF, scale=0.125)
            if j < 3:
                load_pair(4 + j + 1)
            else:
                wloadm_j(0)
            vtrans(lambda i, KL: vT.ap[:, 128 * i:128 * i + 128], 21, lambda i: 128)
            for (G, nr, tiles) in B_GROUPS:
                bu = blk_rr[0] % 2
                blk_rr[0] += 1
                n = 64 * nr
                q0 = 256 * G

                def evac_b(bu=bu, q0=q0, n=n):
                    S.add("scalar", lambda e: e.activation(out=UACC.ap[:, q0:q0 + n], in_=PS[bu][:, 0:n], func=AF.Copy),
                          reads=psk(bu), writes=UACC.k(q0, q0 + n))
                    S.add("scalar", lambda e: e.activation(out=DACC.ap[:, q0:q0 + n], in_=PS[bu][:, 256:256 + n],
                                                           func=AF.Identity, bias=1e-30),
                          reads=psk(bu), writes=DACC.k(q0, q0 + n))
                for ti_, lm in enumerate(tiles):
                    blk0 = 12 - 2 * lm + 4 * G
                    kap = lambda hh, lm=lm: kT.ap[64 * hh:64 * hh + 64, 128 * lm:128 * lm + 128]
                    qap = lambda hh, q0=q0, n=n: qT.ap[64 * hh:64 * hh + 64, q0:q0 + n]
                    tab_ap = tabB[:, :, 64 * blk0:64 * blk0 + n]
                    vi = B_UIDX[(G, lm)]
                    valf = lambda hh, vi=vi, nr=nr: VALB[64 * hh:64 * hh + 64, vi:vi + nr].unsqueeze(2).to_broadcast([64, nr, 64])
                    lastt = ti_ == len(tiles) - 1
                    unit(128, n, kap, qap, tab_ap, None, valf, lm, PS[bu][:, 0:n], PS[bu][:, 256:256 + n], bu,
                         ti_ == 0, lastt, tab.k(), after=([evac_b] if lastt else ()))
            flush()
            S.add("vector", lambda e: e.reciprocal(out=DACC.ap, in_=DACC.ap), reads=DACC.k(), writes=DACC.k())
            S.add("gpsimd", (lambda e, j=j: e.tensor_tensor(out=OAB.ap[:, 4 + j, :], in0=UACC.ap, in1=DACC.ap, op=ALU.mult)),
                  reads=UACC.k() + DACC.k(), writes=OAB.k2(4 + j, 0, NQ))

        wup_v = wup_d.rearrange("(c p) f -> p c f", p=128)
        def wup_load(pos):
            fj, half = pos // 2, pos % 2
            cc = fj + 22 * half
            S.add("gpsimd", lambda e: e.dma_start(out=WUP.ap[:, pos, :, :], in_=wup_v[:, :, 128 * cc:128 * cc + 128]),
                  writes=WUP.k(1024 * pos, 1024 * (pos + 1)), dma=True, semkey="wup%d" % (pos % 4))
        ETMP = [[A2(12288 + 8192 * s_ + 2048 * i, [128, 512], F32) for i in range(4)] for s_ in range(2)]
        WOUT = A2(28672, [128, 8, 1024], BF16)
        wout_v = wout_d.rearrange("(c p) f -> p c f", p=128)
        m1_rr = 0
        for j in range(8):
            wm = WM[j % 2]
            if j + 1 < 8:
                wloadm_j(j + 1)
            if j == 1:
                for hf in range(2):
                    S.add("gpsimd", (lambda e, hf=hf: e.dma_start(out=WOUT.ap[:, :, 512 * hf:512 * hf + 512],
                                                                  in_=wout_v[:, :, 512 * hf:512 * hf + 512])),
                          writes=WOUT.k(), dma=True, semkey="wout")
            if j == 6:
                for pos in range(34, 44):
                    wup_load(pos)
            for (t0, W) in qtiles:
                s_ = m1_rr % 2
                m1_rr += 1
                banks = [4 * s_ + i for i in range(4)]
                et = ETMP[s_]
                for kc in range(8):
                    S.add("tensor", (lambda e, kc=kc, t0=t0, W=W, b=banks[0], wm=wm: e.matmul(
                        PS[b][:, 0:W], lhsT=wm.ap[:, kc, :], rhs=hT.ap[:, kc, QOFF + t0:QOFF + t0 + W],
                        start=(kc == 0), stop=(kc == 7))),
                        reads=wm.k() + hT.k2(kc, QOFF + t0, QOFF + t0 + W), writes=psk(banks[0]))
                for kc in range(8):
                    S.add("tensor", (lambda e, kc=kc, t0=t0, W=W, b=banks[1], wm=wm: e.matmul(
                        PS[b][:, 0:W], lhsT=wm.ap[:, 8 + kc, :], rhs=hT.ap[:, kc, QOFF + t0:QOFF + t0 + W],
                        start=(kc == 0), stop=(kc == 7))),
                        reads=wm.k() + hT.k2(kc, QOFF + t0, QOFF + t0 + W), writes=psk(banks[1]))
                for kc in range(4):
                    S.add("tensor", (lambda e, kc=kc, t0=t0, W=W, b=banks[2], wm=wm: e.matmul(
                        PS[b][:, 0:W], lhsT=wm.ap[:, 16 + kc, :], rhs=OAB.ap[:, kc, t0:t0 + W],
                        start=(kc == 0), stop=(kc == 3))),
                        reads=wm.k() + OAB.k2(kc, t0, t0 + W), writes=psk(banks[2]))
                for kc in range(4):
                    S.add("tensor", (lambda e, kc=kc, t0=t0, W=W, b=banks[3], wm=wm: e.matmul(
                        PS[b][:, 0:W], lhsT=wm.ap[:, 20 + kc, :], rhs=OAB.ap[:, 4 + kc, t0:t0 + W],
                        start=(kc == 0), stop=(kc == 3))),
                        reads=wm.k() + OAB.k2(4 + kc, t0, t0 + W), writes=psk(banks[3]))
                S.add("scalar", (lambda e, W=W, b=banks[0], et=et: e.activation(out=et[0].ap[:, 0:W], in_=PS[b][:, 0:W],
                                                                              func=AF.Sigmoid)),
                      reads=psk(banks[0]), writes=et[0].k())
                S.add("scalar", (lambda e, W=W, b=banks[1], et=et: e.activation(out=et[1].ap[:, 0:W], in_=PS[b][:, 0:W],
                                                                              func=AF.Sigmoid)),
                      reads=psk(banks[1]), writes=et[1].k())
                S.add("vector", (lambda e, W=W, b=banks[2], et=et: e.tensor_tensor(
                    out=et[2].ap[:, 0:W], in0=PS[b][:, 0:W], in1=et[0].ap[:, 0:W], op=ALU.mult)),
                    reads=psk(banks[2]) + et[0].k(), writes=et[2].k())
                S.add("vector", (lambda e, W=W, b=banks[3], et=et: e.tensor_tensor(
                    out=et[3].ap[:, 0:W], in0=PS[b][:, 0:W], in1=et[1].ap[:, 0:W], op=ALU.mult)),
                    reads=psk(banks[3]) + et[1].k(), writes=et[3].k())
                S.add("gpsimd", (lambda e, W=W, et=et, j=j, t0=t0: e.tensor_tensor(
                    out=MERG.ap[:, j, t0:t0 + W], in0=et[2].ap[:, 0:W], in1=et[3].ap[:, 0:W], op=ALU.add)),
                    reads=et[2].k() + et[3].k(), writes=MERG.k2(j, t0, t0 + W))

        for pos in range(17):
            wup_load(pos)
        XT2 = A2(12288, [128, 8, 512], F32)
        m2_rr = 0
        for (t0, W) in qtiles:
            S.add("sync", (lambda e, t0=t0, W=W: e.dma_start(out=XT2.ap[:, :, 0:W], in_=xT_v[:, :, QOFF + t0:QOFF + t0 + W])),
                  writes=XT2.k(), dma=True, semkey="xt0")
            for j in range(8):
                bank = m2_rr % 8
                m2_rr += 1
                for kc in range(8):
                    S.add("tensor", (lambda e, kc=kc, t0=t0, W=W, bank=bank, j=j: e.matmul(
                        PS[bank][:, 0:W], lhsT=WOUT.ap[:, kc, 128 * j:128 * j + 128], rhs=MERG.ap[:, kc, t0:t0 + W],
                        start=(kc == 0), stop=(kc == 7))),
                        reads=WOUT.k() + MERG.k2(kc, t0, t0 + W), writes=psk(bank))
                S.add("vector", (lambda e, t0=t0, W=W, bank=bank, j=j: e.scalar_tensor_tensor(
                    out=xmid.ap[:, j, t0:t0 + W], in0=PS[bank][:, 0:W], scalar=gate1(j), in1=XT2.ap[:, j, 0:W],
                    op0=ALU.mult, op1=ALU.add)),
                    reads=psk(bank) + XT2.k2(j, 0, 512) + ["ADATb"], writes=xmid.k2(j, t0, t0 + W))

        wup_v = wup_d.rearrange("(c p) f -> p c f", p=128)
        wdn_v = wdn_d.rearrange("(f p) m -> p f m", p=128)
        yT_v = yT_d.rearrange("(c p) t -> p c t", p=128)
        WDN = [A2(0, [128, 22, 128], BF16), A2(5632, [128, 22, 128], BF16)]
        GTB = A2(11264, [128, 22, FW], BF16)
        H2 = A2(26624, [128, 8, FW], BF16)
        HB = A2(26624 + 5504, [128, 8, 2], BF16)
        TV = [A2(32256 + 1536 * i, [128, FW], F32) for i in range(3)]
        TG = [A2(32256 + 1536 * (3 + i), [128, FW], F32) for i in range(3)]
        GG = [A2(32256 + 1536 * (6 + i), [128, FW], F32) for i in range(2)]
        RS2 = A2(44544, [128, FW], F32)
        TMPF = A2(46080, [128, FW], F32)
        SQC = [A2(46080, [128, FW], BF16), A2(46080 + 768, [128, FW], BF16)]
        YB = [A2(47616, [128, FW], F32), A2(47616, [128, FW], F32)]

        for pos in range(17, 34):
            wup_load(pos)
        zt = []
        zc = 63
        while zc + 1 < 2112:
            W = min(FW, 2113 - zc)
            zt.append((zc, W))
            zc += W - 2
        wdn_rr = [0]
        ffn_rr = 0
        y_rr = 0

        cur_tile = [1]

        def wdn_load(j):
            b = wdn_rr[0] % 2
            wd = WDN[b]
            wdf = A2(5632 * b, [128, 22 * 128], BF16)
            if cur_tile[0] == 0:
                S.add("gpsimd", (lambda e, wd=wd, j=j: e.dma_start(out=wd.ap, in_=wdn_v[:, :, 128 * j:128 * j + 128])),
                      writes=wd.k(), dma=True, semkey="wdn%d" % b)
                S.add("sync", (lambda e, wdf=wdf, j=j: e.dma_start(out=wdn_s[j], in_=wdf.ap)),
                      reads=wd.k(), writes=[("wdn_s", j)], dma=True, semkey="wds%d" % b)
            else:
                S.add("sync", (lambda e, wdf=wdf, j=j: e.dma_start(out=wdf.ap, in_=wdn_s[j])),
                      reads=[("wdn_s", j)], writes=wd.k(), dma=True, semkey="wdq%d" % b)
            wdn_rr[0] += 1
            return wd

        def h2_tile(ti, part=0):
            zc0, W = zt[ti]
            cs = 0 if ti == 0 else 2
            if part in (0, 1):
                h2_part1(ti, zc0, W, cs)
            if part in (0, 2):
                h2_part2(ti, zc0, W, cs)

        def h2_part1(ti, zc0, W, cs):
            for c in range(8):
                sq = SQC[c % 2]
                S.add("scalar", (lambda e, c=c, sq=sq: e.activation(
                    out=sq.ap[:, cs:W], in_=xmid.ap[:, c, zc0 + cs:zc0 + W], func=AF.Square)),
                    reads=xmid.k2(c, zc0 + cs, zc0 + W), writes=sq.k())
                S.add("tensor", (lambda e, c=c, sq=sq: e.matmul(PS[0][:, cs:W], lhsT=ONES[:], rhs=sq.ap[:, cs:W],
                                                              start=(c == 0), stop=(c == 7))),
                      reads=["ONES"] + sq.k(), writes=psk(0))
            S.add("scalar", lambda e: e.activation(out=RS2.ap[:, cs:W], in_=PS[0][:, cs:W], func=AF.Sqrt,
                                                   bias=1e-6, scale=1.0 / 1024.0),
                  reads=psk(0), writes=RS2.k())
            S.add("vector", lambda e: e.reciprocal(out=RS2.ap[:, cs:W], in_=RS2.ap[:, cs:W]),
                  reads=RS2.k(), writes=RS2.k())

        def h2_part2(ti, zc0, W, cs):
            if ti > 0:
                S.add("gpsimd", lambda e: e.tensor_copy(out=H2.ap[:, :, 0:2], in_=HB.ap), reads=HB.k(), writes=H2.k())
            for c in range(8):
                S.add("vector", (lambda e, c=c: e.scalar_tensor_tensor(
                    out=TMPF.ap[:, cs:W], in0=xmid.ap[:, c, zc0 + cs:zc0 + W], scalar=a2(c), in1=RS2.ap[:, cs:W],
                    op0=ALU.mult, op1=ALU.mult)),
                    reads=xmid.k2(c, zc0 + cs, zc0 + W) + RS2.k() + ["COL2"], writes=TMPF.k())
                S.add("scalar", (lambda e, c=c: e.activation(
                    out=H2.ap[:, c, cs:W], in_=TMPF.ap[:, cs:W], func=AF.Identity, bias=b2(c))),
                    reads=TMPF.k() + ["ADATb"], writes=H2.k2(c, 0, FW))
            if ti == 0:
                S.add("gpsimd", lambda e: e.tensor_scalar(out=H2.ap[:, :, 0:1], in0=H2.ap[:, :, 0:1], scalar1=ZM[:, 0:1],
                                                          scalar2=None, op0=ALU.mult), reads=H2.k() + ["ZM"], writes=H2.k())
            if ti == len(zt) - 1:
                S.add("gpsimd", lambda e: e.tensor_scalar(out=H2.ap[:, :, W - 1:W], in0=H2.ap[:, :, W - 1:W],
                                                          scalar1=ZM[:, 1:2], scalar2=None, op0=ALU.mult),
                      reads=H2.k() + ["ZM"], writes=H2.k())
            else:
                S.add("gpsimd", lambda e: e.tensor_copy(out=HB.ap, in_=H2.ap[:, :, W - 2:W]),
                      reads=H2.k(), writes=HB.k())

        h2_tile(0)
        for ti, (zc0, W) in enumerate(zt):
            nout = W - 2
            wds = [wdn_load(0), wdn_load(1)]
            stage_b = []
            fj_order = list(range(22)) if ti > 0 else list(range(0, 8)) + list(range(17, 22)) + list(range(8, 17))
            for fj in fj_order:
                u = ffn_rr % 2
                u3 = ffn_rr % 3
                ffn_rr += 1
                bv, bg = 2 * u3, 2 * u3 + 1
                for half, bank in ((0, bv), (1, bg)):
                    pos = 2 * fj + half
                    for kc in range(8):
                        S.add("tensor", (lambda e, kc=kc, pos=pos, bank=bank, W=W: e.matmul(
                            PS[bank][:, 0:W], lhsT=WUP.ap[:, pos, kc, :], rhs=H2.ap[:, kc, 0:W],
                            start=(kc == 0), stop=(kc == 7))),
                            reads=WUP.k(1024 * pos + 128 * kc, 1024 * pos + 128 * kc + 128) + H2.k2(kc, 0, FW),
                            writes=psk(bank))
                gg, tv, tg = GG[u], TV[u3], TG[u3]
                cv, cg = fj, fj + 22
                if stage_b:
                    stage_b[0][0]()
                S.add("scalar", (lambda e, cc=cv, bank=bv, tb=tv, nout=nout: e.activation(
                    out=tb.ap[:, 0:nout], in_=PS[bank][:, 1:1 + nout], func=AF.Identity,
                    bias=CONV[:, 3, cc:cc + 1], scale=CONV[:, 1, cc:cc + 1])),
                    reads=psk(bv) + ["CONV"], writes=tv.k())
                S.add("scalar", (lambda e, cc=cg, bank=bg, tb=tg, nout=nout: e.activation(
                    out=tb.ap[:, 0:nout], in_=PS[bank][:, 1:1 + nout], func=AF.Identity,
                    bias=CONV[:, 3, cc:cc + 1], scale=CONV[:, 1, cc:cc + 1])),
                    reads=psk(bg) + ["CONV"], writes=tg.k())
                S.add("scalar", (lambda e, cc=cg, bank=bg, gg=gg, nout=nout: e.activation(
                    out=gg.ap[:, 0:nout], in_=PS[bank][:, 0:nout], func=AF.Identity, scale=CONV[:, 0, cc:cc + 1])),
                    reads=psk(bg) + ["CONV"], writes=gg.k())
                S.add("vector", (lambda e, cc=cv, bank=bv, tb=tv, nout=nout: e.scalar_tensor_tensor(
                    out=tb.ap[:, 0:nout], in0=PS[bank][:, 0:nout], scalar=CONV[:, 0, cc:cc + 1], in1=tb.ap[:, 0:nout],
                    op0=ALU.mult, op1=ALU.add)), reads=psk(bv) + ["CONV"] + tv.k(), writes=tv.k())
                S.add("vector", (lambda e, cc=cv, bank=bv, tb=tv, nout=nout: e.scalar_tensor_tensor(
                    out=tb.ap[:, 0:nout], in0=PS[bank][:, 2:2 + nout], scalar=CONV[:, 2, cc:cc + 1], in1=tb.ap[:, 0:nout],
                    op0=ALU.mult, op1=ALU.add)), reads=psk(bv) + ["CONV"] + tv.k(), writes=tv.k())
                S.add("gpsimd", (lambda e, tb=tg, gg=gg, nout=nout: e.tensor_tensor(
                    out=tb.ap[:, 0:nout], in0=tb.ap[:, 0:nout], in1=gg.ap[:, 0:nout], op=ALU.add)),
                    reads=tg.k() + gg.k(), writes=tg.k())

                def sb_dve(cg=cg, bg=bg, tg=tg, nout=nout):
                    S.add("vector", lambda e: e.scalar_tensor_tensor(
                        out=tg.ap[:, 0:nout], in0=PS[bg][:, 2:2 + nout], scalar=CONV[:, 2, cg:cg + 1], in1=tg.ap[:, 0:nout],
                        op0=ALU.mult, op1=ALU.add), reads=psk(bg) + ["CONV"] + tg.k(), writes=tg.k())

                def sb(fj=fj, tv=tv, tg=tg, nout=nout):
                    S.add("scalar", lambda e: e.activation(out=tg.ap[:, 0:nout], in_=tg.ap[:, 0:nout],
                                                           func=AF.Gelu_apprx_tanh), reads=tg.k(), writes=tg.k())
                    S.add("gpsimd", lambda e: e.tensor_tensor(out=GTB.ap[:, fj, 0:nout], in0=tg.ap[:, 0:nout],
                                                              in1=tv.ap[:, 0:nout], op=ALU.mult),
                          reads=tg.k() + tv.k(), writes=GTB.k2(fj, 0, FW))
                if stage_b:
                    stage_b.pop(0)[1]()
                stage_b.append((sb_dve, sb))
            while stage_b:
                d_, r_ = stage_b.pop(0)
                d_()
                r_()
            oc0 = zc0 + 1
            for j in range(8):
                wd = wds[j]
                bank = 6 + (j % 2)
                for fj in range(22):
                    S.add("tensor", (lambda e, fj=fj, wd=wd, bank=bank, nout=nout: e.matmul(
                        PS[bank][:, 0:nout], lhsT=wd.ap[:, fj, :], rhs=GTB.ap[:, fj, 0:nout],
                        start=(fj == 0), stop=(fj == 21))),
                        reads=wd.k() + GTB.k2(fj, 0, FW), writes=psk(bank))
                if j + 2 < 8:
                    wds.append(wdn_load(j + 2))
                S.add("vector", (lambda e, j=j, bank=bank, nout=nout, oc0=oc0: e.scalar_tensor_tensor(
                    out=xmid.ap[:, j, oc0:oc0 + nout], in0=PS[bank][:, 0:nout], scalar=gate2(j),
                    in1=xmid.ap[:, j, oc0:oc0 + nout], op0=ALU.mult, op1=ALU.add)),
                    reads=psk(bank) + xmid.k2(j, oc0, oc0 + nout) + ["ADATb"], writes=xmid.k2(j, oc0, oc0 + nout))
                if ti + 1 < len(zt) and j == 1:
                    h2_tile(ti + 1, part=1)
                if ti + 1 < len(zt) and j == 3:
                    h2_tile(ti + 1, part=2)
            for c in range(8):
                sq = SQC[c % 2]
                S.add("scalar", (lambda e, c=c, sq=sq, oc0=oc0, nout=nout: e.activation(
                    out=sq.ap[:, 0:nout], in_=xmid.ap[:, c, oc0:oc0 + nout], func=AF.Square)),
                    reads=xmid.k2(c, oc0, oc0 + nout), writes=sq.k())
                S.add("tensor", (lambda e, c=c, sq=sq, nout=nout: e.matmul(PS[6][:, 0:nout], lhsT=ONES[:], rhs=sq.ap[:, 0:nout],
                                                                            start=(c == 0), stop=(c == 7))),
                      reads=["ONES"] + sq.k(), writes=psk(6))
            S.add("scalar", (lambda e, nout=nout: e.activation(out=RS2.ap[:, 0:nout], in_=PS[6][:, 0:nout], func=AF.Sqrt,
                                                               bias=1e-6, scale=1.0 / 1024.0)),
                  reads=psk(6), writes=RS2.k())
            S.add("vector", (lambda e, nout=nout: e.reciprocal(out=RS2.ap[:, 0:nout], in_=RS2.ap[:, 0:nout])),
                  reads=RS2.k(), writes=RS2.k())
            tk0 = oc0 - 64
            for c in range(8):
                yb = (TV + TG + GG)[c]
                ysem = "yo%d" % (y_rr % 4)
                y_rr += 1
                S.add("vector", (lambda e, c=c, yb=yb, oc0=oc0, nout=nout: e.scalar_tensor_tensor(
                    out=yb.ap[:, 0:nout], in0=xmid.ap[:, c, oc0:oc0 + nout], scalar=gfin(c), in1=RS2.ap[:, 0:nout],
                    op0=ALU.mult, op1=ALU.mult)),
                    reads=xmid.k2(c, oc0, oc0 + nout) + RS2.k() + ["GTC"], writes=yb.k())
                S.add("sync", (lambda e, c=c, yb=yb, tk0=tk0, nout=nout: e.dma_start(
                    out=yT_v[:, c, tk0:tk0 + nout], in_=yb.ap[:, 0:nout])),
                    reads=yb.k(), dma=True, semkey=ysem)

        S.emit(nc, st)
    return nc


_WUP_ALIAS_FIX = True


def _host_tables(rel_bias, na_rpb):
    p = np.arange(128)[:, None]
    c = np.arange(256)[None, :]
    delta = p - c + 64
    band = np.abs(delta) <= 64
    GA = np.full((128, 3, 8, 256), NEG, dtype=np.float32)
    for g, d in enumerate(DIL):
        bk = _t5_bucket(delta * d)
        vals = rel_bias[bk]
        vals = np.where(band[:, :, None], vals, np.float32(NEG))
        GA[:, g, :, :] = np.transpose(vals, (0, 2, 1))
    TB = np.zeros((128, 8, 1024), dtype=np.float32)
    kc = np.arange(64)[:, None]
    qc = np.arange(64)[None, :]
    cstart = np.clip(qc - 8, 0, 48)
    colok = (kc >= cstart) & (kc < cstart + 16)
    dc = np.clip(kc - qc, -15, 15) + 15
    for blk in range(16):
        e = 7 - blk
        for half in range(2):
            dr = e + half
            if 0 <= dr + 7 <= 14:
                v = na_rpb[:, dr + 7, :][:, dc]
            else:
                v = np.zeros((8, 64, 64), dtype=np.float32)
            v = np.where(colok[None], v, np.float32(NEG))
            TB[64 * half:64 * half + 64, :, 64 * blk:64 * blk + 64] = np.transpose(v, (1, 0, 2))
    return GA, TB


def _core_masks(cidx):
    T0 = TOK * cidx
    R0 = 32 * cidx
    km = np.zeros((128, NKT), dtype=np.float32)
    pp = np.arange(128)
    for g, (d, jq0, nqc, nt, blocks, tl) in enumerate(A_GEOM):
        for r in range(d):
            for t in range(nt):
                kk = jq0 - 64 + 128 * t + pp
                glob = T0 - SOFF + r + d * kk
                ok = (glob >= 0) & (glob < S_TOT)
                km[:, A_TIDX[(g, r, t)]] = np.where(ok, 0.0, NEG)
    val = np.zeros((128, NVB), dtype=np.float32)
    for G, nr, tiles in B_GROUPS:
        for lm in tiles:
            vi = B_UIDX[(G, lm)]
            for jj in range(nr):
                rr = R0 - 1 + 4 * G + jj
                for half in range(2):
                    kr = R0 - 6 + 2 * lm + half
                    if 0 <= rr < 256:
                        rs = min(max(rr - 4, 0), 248)
                        ok = (rs <= kr < rs + 8)
                    else:
                        ok = (0 <= kr < 256) and (rr - 4 <= kr < rr + 4)
                    v = 0.0 if ok else NEG
                    val[half, vi + jj] = v
                    val[64 + half, vi + jj] = v
    zm = np.ones((128, 2), dtype=np.float32)
    if cidx == 0:
        zm[:, 0] = 0.0
    if cidx == NCORE - 1:
        zm[:, 1] = 0.0
    return km, val.astype(ml_dtypes.bfloat16), zm


def kernel(x, c, w_ada, b_ada, g_mix, w_in, rel_bias, na_rpb, w_branch_dil, w_branch_na,
           w_out, g_ffn, w_up, conv_w, conv_b, w_down, g_final):
    f32 = np.float32
    x = np.asarray(x, dtype=f32)
    xT = np.ascontiguousarray(x[0].T)
    xTp = np.zeros((1024, S_TOT + 2 * SOFF), dtype=f32)
    xTp[:, SOFF:SOFF + S_TOT] = xT
    colmaj = lambda v, n: np.ascontiguousarray(np.asarray(v, dtype=f32).reshape(n, 128).T)
    cT = colmaj(c[0], 8)
    badaT = colmaj(b_ada[0], 48)
    gT = np.concatenate([colmaj(g_mix[0], 8), colmaj(g_ffn[0], 8), colmaj(g_final, 8)], axis=1)
    convT = np.zeros((128, 4, 44), dtype=f32)
    for k in range(3):
        convT[:, k, :] = colmaj(conv_w[0, k], 44)
    convT[:, 3, :] = colmaj(conv_b[0], 44)
    GA, TB = _host_tables(np.asarray(rel_bias, dtype=f32), np.asarray(na_rpb[0], dtype=f32))
    indB = np.zeros((128, 128), dtype=f32)
    for base in (0, 64):
        indB[base, 0:64] = 1.0
        indB[base + 1, 64:128] = 1.0
    indB = indB.astype(ml_dtypes.bfloat16)
    ident = np.eye(128, dtype=f32).astype(ml_dtypes.bfloat16)
    shared = {
        "cT": cT, "wada": np.ascontiguousarray(w_ada[0], dtype=f32), "badaT": badaT, "gT": gT,
        "w_in": np.ascontiguousarray(w_in[0], dtype=f32), "w_bd": np.ascontiguousarray(w_branch_dil[0], dtype=f32),
        "w_bn": np.ascontiguousarray(w_branch_na[0], dtype=f32), "w_out": np.ascontiguousarray(w_out[0], dtype=f32),
        "w_up": np.ascontiguousarray(w_up[0], dtype=f32), "convT": convT,
        "w_down": np.ascontiguousarray(w_down[0], dtype=f32), "GA": GA, "TBt": TB, "indB": indB, "ident": ident,
    }
    in_maps = []
    for ci in range(NCORE):
        km, val, zm = _core_masks(ci)
        m = dict(shared)
        m["xT"] = np.ascontiguousarray(xTp[:, TOK * ci:TOK * ci + SL])
        m["kmaskA"] = km
        m["valB"] = val
        m["zmask"] = zm
        in_maps.append(m)
    nc = build_program()
    res = run_bass_kernel_spmd(nc, in_maps, core_ids=list(range(NCORE)))
    yT = np.concatenate([np.asarray(r["yT"], dtype=f32) for r in res.results], axis=1)
    return np.ascontiguousarray(yT.T)[None].astype(f32)
```

```python
import math
from contextlib import ExitStack

import numpy as np
import ml_dtypes

import concourse.bass as bass
import concourse.mybir as mybir
from concourse.bass_utils import run_bass_kernel_spmd

F32 = mybir.dt.float32
BF16 = mybir.dt.bfloat16
AF = mybir.ActivationFunctionType
ALU = mybir.AluOpType

S_TOT = 16384
NCORE = 8
TOK = 2048
SL = 4224
SOFF = 1088
QOFF = 1024
NQ = 2176
BKOFF = 704
NBK = 2816
D_FF = 2816
NEG = -30000.0
DIL = (1, 4, 16)
FW = 344
GRAN = 256

ENGS = ("sync", "gpsimd", "scalar", "vector", "tensor")


class Op:
    __slots__ = ("eng", "fn", "dma", "semkey", "ndma", "deps", "sigval", "need_sig", "idx")


class Sched:
    def __init__(self):
        self.ops = {e: [] for e in ENGS}
        self.last_w = {}
        self.readers = {}
        self.dma_last = {}
        self.dma_count = {}
        self.all_ops = []

    def add(self, eng, fn, reads=(), writes=(), dma=False, semkey=None, ndma=1):
        op = Op()
        op.eng, op.fn, op.dma, op.semkey, op.ndma = eng, fn, dma, semkey, ndma
        op.need_sig = dma
        op.sigval = None
        deps = {}
        for k in reads:
            w = self.last_w.get(k)
            if w is not None:
                deps[id(w)] = (w, True)
            if isinstance(k, tuple) and k[0] == "ps":
                for r in self.readers.get(k, ()):
                    if r.eng != eng and id(r) not in deps:
                        deps[id(r)] = (r, True)
        for k in writes:
            w = self.last_w.get(k)
            if w is not None and id(w) not in deps:
                deps[id(w)] = (w, False)
            for r in self.readers.get(k, ()):
                if id(r) not in deps:
                    deps[id(r)] = (r, False)
        if dma:
            prev = self.dma_last.get(semkey)
            if prev is not None:
                deps[id(prev)] = (prev, True)
            self.dma_last[semkey] = op
            self.dma_count[semkey] = self.dma_count.get(semkey, 0) + ndma
            op.sigval = 16 * self.dma_count[semkey]
        op.deps = []
        for w, raw in deps.values():
            if w is op:
                continue
            if (not w.dma) and (not dma) and w.eng == eng and not raw and eng == "tensor":
                continue
            op.deps.append(w)
            w.need_sig = True
        for k in reads:
            self.readers.setdefault(k, []).append(op)
        for k in writes:
            self.last_w[k] = op
            self.readers[k] = []
        op.idx = len(self.all_ops)
        self.all_ops.append(op)
        self.ops[eng].append(op)
        return op

    def emit(self, nc, st):
        eng_sem = {e: st.enter_context(nc.semaphore("se_" + e)) for e in ENGS}
        dma_sem = {k: st.enter_context(nc.semaphore("sd_%d" % i)) for i, k in enumerate(self.dma_count)}
        for e in ENGS:
            n = 0
            for op in self.ops[e]:
                if not op.dma and op.need_sig:
                    n += 1
                    op.sigval = n
        block = st.enter_context(nc.Block())

        def run(ename, e):
            waited = {}
            for op in self.ops[ename]:
                for w in op.deps:
                    sem = dma_sem[w.semkey] if w.dma else eng_sem[w.eng]
                    if waited.get(sem.num, 0) >= w.sigval:
                        continue
                    waited[sem.num] = w.sigval
                    e.wait_ge(sem, w.sigval)
                r = op.fn(e)
                if op.dma:
                    rs = r if isinstance(r, (list, tuple)) else [r]
                    assert len(rs) == op.ndma
                    for ins in rs:
                        ins.then_inc(dma_sem[op.semkey], 16)
                elif op.need_sig:
                    r.then_inc(eng_sem[ename], 1)
            if ename == "sync":
                for k, cnt in self.dma_count.items():
                    e.wait_ge(dma_sem[k], 16 * cnt)

        block.sync(lambda e: run("sync", e))
        block.gpsimd(lambda e: run("gpsimd", e))
        block.scalar(lambda e: run("scalar", e))
        block.vector(lambda e: run("vector", e))
        block.tensor(lambda e: run("tensor", e))


class Buf:
    def __init__(self, arena_name, arena_ap, off, shape, dtype):
        self.an, self.off, self.shape, self.dtype = arena_name, off, tuple(shape), dtype
        self.esz = 4 if dtype == F32 else 2
        n = 1
        for s in shape[1:]:
            n *= s
        self.n = n
        assert off % 4 == 0
        a = arena_ap[:, off // 2: off // 2 + n * self.esz // 2]
        if dtype == F32:
            a = a.bitcast(F32)
        if len(shape) == 3:
            a = a.rearrange("p (a b) -> p a b", a=shape[1])
        elif len(shape) == 4:
            a = a.rearrange("p (a b c) -> p a b c", a=shape[1], b=shape[2])
        self.ap = a

    def k(self, lo=0, hi=None):
        if hi is None:
            hi = self.n
        b0 = (self.off + lo * self.esz) // GRAN
        b1 = (self.off + hi * self.esz - 1) // GRAN
        return [(self.an, g) for g in range(b0, b1 + 1)]

    def k2(self, c, lo, hi):
        inner = self.shape[2] if len(self.shape) == 3 else self.shape[2] * self.shape[3]
        return self.k(c * inner + lo, c * inner + hi)


def _t5_bucket(rel):
    rel = np.asarray(rel, dtype=np.int64)
    n = 16
    max_exact = 8
    sign_part = np.where(rel > 0, n, 0)
    a = np.abs(rel)
    af = np.maximum(a, 1).astype(np.float32)
    large = max_exact + (np.log(af / np.float32(max_exact)) / np.float32(math.log(1024 / max_exact))
                         * np.float32(n - max_exact)).astype(np.int32)
    large = np.minimum(large, n - 1)
    return sign_part + np.where(a < max_exact, a, large)


def _a_geom():
    out = []
    for d in DIL:
        nqc = NQ // d
        jq0 = QOFF // d
        span = nqc + 128
        nt = (span + 127) // 128
        tl = [min(128, span - 128 * t) for t in range(nt)]
        blocks = []
        b = 0
        while 256 * b < nqc:
            blocks.append((b, min(256, nqc - 256 * b)))
            b += 1
        out.append((d, jq0, nqc, nt, blocks, tl))
    return out


A_GEOM = _a_geom()


def _a_tile_index():
    idx = {}
    n = 0
    for g, (d, jq0, nqc, nt, blocks, tl) in enumerate(A_GEOM):
        for r in range(d):
            for t in range(nt):
                idx[(g, r, t)] = n
                n += 1
    return idx, n


A_TIDX, NKT = _a_tile_index()


def _b_groups():
    gs = []
    for G in range(9):
        nr = min(4, 34 - 4 * G)
        lo, hi = 2 * G, min(2 * G + 5, 20)
        if G == 0:
            hi = 6
        if G == 8:
            lo = 15
        gs.append((G, nr, list(range(lo, hi + 1))))
    return gs


B_GROUPS = _b_groups()


def _b_unit_index():
    idx = {}
    n = 0
    for G, nr, tiles in B_GROUPS:
        for lm in tiles:
            idx[(G, lm)] = n
            n += 4
    return idx, n


B_UIDX, NVB = _b_unit_index()


def build_program():
    nc = bass.Bass("TRN2", target_bir_lowering=False)
    din = lambda name, shape, dt=F32: nc.dram_tensor(name, list(shape), dt, kind="ExternalInput").ap()
    xT_d = din("xT", [1024, SL])
    cT_d = din("cT", [128, 8])
    wada_d = din("wada", [1024, 6144])
    bada_d = din("badaT", [128, 48])
    gT_d = din("gT", [128, 24])
    win_d = din("w_in", [1024, 5120])
    wbd_d = din("w_bd", [512, 1024])
    wbn_d = din("w_bn", [512, 1024])
    wout_d = din("w_out", [1024, 1024])
    wup_d = din("w_up", [1024, 5632])
    conv_d = din("convT", [128, 4, 44])
    wdn_d = din("w_down", [D_FF, 1024])
    ga_d = din("GA", [128, 3, 8, 256])
    tb_d = din("TBt", [128, 8, 1024])
    kma_d = din("kmaskA", [128, NKT])
    valb_d = din("valB", [128, NVB], BF16)
    indb_d = din("indB", [128, 128], BF16)
    ident_d = din("ident", [128, 128], BF16)
    zm_d = din("zmask", [128, 2])
    yT_d = nc.dram_tensor("yT", [1024, TOK], F32, kind="ExternalOutput").ap()

    S = Sched()
    with ExitStack() as st:
        sb = lambda name, shape, dt: st.enter_context(nc.sbuf_tensor(name, list(shape), dt))
        BIG_T = sb("BIG", [128, 69632 // 2], BF16)
        AR_T = sb("ARENA", [128, 90112 // 2], BF16)
        A2_T = sb("A2", [128, 49152 // 2], BF16)
        CT = sb("CT", [128, 8], F32)
        SCT = sb("SCT", [128, 8], BF16)
        BADA = sb("BADA", [128, 48], F32)
        GTC = sb("GTC", [128, 24], F32)
        CONV = sb("CONV", [128, 4, 44], F32)
        KMA = sb("KMA", [128, NKT], F32)
        VALB = sb("VALB", [128, NVB], BF16)
        INDB = sb("INDB", [128, 128], BF16)
        IDENT = sb("IDENT", [128, 128], BF16)
        ZM = sb("ZM", [128, 2], F32)
        ADAT = sb("ADAT", [128, 48], F32)
        COL = sb("COL", [128, 16], F32)
        TMPC = sb("TMPC", [128, 16], F32)
        ONES = sb("ONES", [128, 128], BF16)
        ONESA = sb("ONESA", [128, 128], BF16)
        ONESB = sb("ONESB", [128, 128], BF16)
        ONE11 = sb("ONE11", [1, 2], F32)
        PS_T = st.enter_context(nc.psum_tensor("PS", [128, 8, 512], F32))
        PS = [PS_T[:, i, :] for i in range(8)]
        psk = lambda i: [("ps", i)]

        BIG = lambda off, shape, dt: Buf("BIG", BIG_T, off, shape, dt)
        AR = lambda off, shape, dt: Buf("AR", AR_T, off, shape, dt)
        A2 = lambda off, shape, dt: Buf("A2", A2_T, off, shape, dt)

        hT = BIG(0, [128, 8, SL], BF16)
        xmid = BIG(0, [128, 8, NQ], F32)
        OAB = AR(0, [128, 8, NQ], BF16)
        kT = AR(34816, [128, SL], BF16)
        qT = AR(43264, [128, NQ], BF16)
        vT = AR(47616, [128, SL], BF16)
        VB = AR(56064, [128, 24, 128], BF16)
        UACC = AR(68352, [128, NQ], F32)
        DACC = AR(77056, [128, NQ], F32)
        MERG = AR(34816, [128, 8, NQ], BF16)
        WUP = AR(0, [128, 44, 8, 128], BF16)
        ADAROW = AR(32768, [128, 6144], F32)

        cq = [0]

        def cdma(out_ap, in_ap, wkeys):
            k = "c%d" % (cq[0] % 4)
            cq[0] += 1
            S.add("sync", lambda e: e.dma_start(out=out_ap, in_=in_ap), writes=wkeys, dma=True, semkey=k)

        cdma(CT[:], cT_d, ["CT"])
        cdma(BADA[:], bada_d, ["BADA"])
        cdma(GTC[:], gT_d, ["GTC"])
        cdma(CONV[:], conv_d, ["CONV"])
        cdma(KMA[:], kma_d, ["KMA"])
        cdma(VALB[:], valb_d, ["VALB"])
        cdma(INDB[:], indb_d, ["INDB"])
        cdma(IDENT[:], ident_d, ["IDENT"])
        cdma(ZM[:], zm_d, ["ZM"])
        S.add("gpsimd", lambda e: e.memset(ONES[:], 1.0), writes=["ONES"])
        S.add("gpsimd", lambda e: e.memset(ONESA[:, 64:128], 0.0), writes=["ONESA0"])
        S.add("gpsimd", lambda e: e.memset(ONESA[:, 0:64], 1.0), writes=["ONESA"])
        S.add("gpsimd", lambda e: e.memset(ONESB[:, 0:64], 0.0), writes=["ONESB0"])
        S.add("gpsimd", lambda e: e.memset(ONESB[:, 64:128], 1.0), writes=["ONESB"])
        S.add("gpsimd", lambda e: e.memset(ONE11[:], 1.0), writes=["ONE11"])
        S.add("scalar", lambda e: e.activation(out=SCT[:], in_=CT[:], func=AF.Silu), reads=["CT"], writes=["SCT"])

        wada_v = wada_d.rearrange("(c p) f -> p c f", p=128)
        WADA = [AR(57344, [128, 8, 1024], BF16), AR(73728, [128, 8, 1024], BF16)]
        for i in range(2):
            wb = WADA[i]
            S.add("gpsimd", (lambda e, wb=wb, i=i: e.dma_start(out=wb.ap, in_=wada_v[:, :, 1024 * i:1024 * (i + 1)])),
                  writes=wb.k(), dma=True, semkey="wada%d" % (i % 2))
            for jj in range(8):
                j = 8 * i + jj
                for kc in range(8):
                    S.add("tensor", (lambda e, wb=wb, kc=kc, jj=jj, j=j: e.matmul(
                        PS[0][:, j:j + 1], lhsT=wb.ap[:, kc, 128 * jj:128 * jj + 128], rhs=SCT[:, kc:kc + 1],
                        start=(kc == 0), stop=(kc == 7))),
                        reads=["SCT"] + wb.k2(kc, 128 * jj, 128 * jj + 128), writes=psk(0))
        S.add("vector", lambda e: e.tensor_tensor(out=ADAT[:, 0:16], in0=PS[0][:, 0:16], in1=BADA[:, 0:16], op=ALU.add),
              reads=psk(0) + ["BADA"], writes=["ADAT"])
        S.add("vector", lambda e: e.tensor_scalar(out=TMPC[:, 0:8], in0=ADAT[:, 8:16], scalar1=1.0, scalar2=None,
                                                  op0=ALU.add), reads=["ADAT"], writes=["TMPC"])
        S.add("vector", lambda e: e.tensor_tensor(out=COL[:, 0:8], in0=TMPC[:, 0:8], in1=GTC[:, 0:8], op=ALU.mult),
              reads=["TMPC", "GTC"], writes=["COL"])
        WADH = AR(26112, [128, 8, 512], BF16)

        def ada_def_load(p):
            S.add("gpsimd", lambda e: e.dma_start(out=WADH.ap, in_=wada_v[:, :, 2048 + 512 * p:2048 + 512 * (p + 1)]),
                  writes=WADH.k(), dma=True, semkey="wadh")

        def ada_def_mm(p):
            for jj in range(4):
                for kc in range(8):
                    S.add("tensor", (lambda e, kc=kc, jj=jj: e.matmul(
                        PS[7][:, jj:jj + 1], lhsT=WADH.ap[:, kc, 128 * jj:128 * jj + 128], rhs=SCT[:, kc:kc + 1],
                        start=(kc == 0), stop=(kc == 7))),
                        reads=["SCT"] + WADH.k2(kc, 128 * jj, 128 * jj + 128), writes=psk(7))
            c0 = 16 + 4 * p
            S.add("vector", lambda e: e.tensor_tensor(out=ADAT[:, c0:c0 + 4], in0=PS[7][:, 0:4], in1=BADA[:, c0:c0 + 4],
                                                      op=ALU.add), reads=psk(7) + ["BADA"], writes=["ADATb"])
            if p == 7:
                S.add("vector", lambda e: e.tensor_scalar(out=TMPC[:, 8:16], in0=ADAT[:, 32:40], scalar1=1.0, scalar2=None,
                                                          op0=ALU.add), reads=["ADATb"], writes=["TMPC2"])
                S.add("vector", lambda e: e.tensor_tensor(out=COL[:, 8:16], in0=TMPC[:, 8:16], in1=GTC[:, 8:16],
                                                          op=ALU.mult), reads=["TMPC2", "GTC"], writes=["COL2"])
        a1 = lambda c: COL[:, c:c + 1]
        b1 = lambda c: ADAT[:, c:c + 1]
        gate1 = lambda c: ADAT[:, 16 + c:17 + c]
        b2 = lambda c: ADAT[:, 24 + c:25 + c]
        a2 = lambda c: COL[:, 8 + c:9 + c]
        gate2 = lambda c: ADAT[:, 40 + c:41 + c]
        gfin = lambda c: GTC[:, 16 + c:17 + c]

        VBF = AR(56064, [128, 6144], BF16)
        slab_tiles = [(512 * i, 512) for i in range(8)] + [(4096, 128)]
        win_v = win_d.rearrange("(c p) f -> p c f", p=128)
        WCH = [A2(0, [128, 8, 384], BF16), A2(6144, [128, 8, 384], BF16)]
        TAB = [A2(12288, [128, 2048], F32), A2(20480, [128, 2048], F32)]
        TT = [A2(28672 + 2048 * i, [128, 2, 256], F32) for i in range(6)]
        PP = [A2(40960 + 1024 * i, [128, 2, 256], BF16) for i in range(6)]
        pending = []
        DEPTH = 5

        def flush():
            while pending:
                pending.pop(0)()
        evac_rr = [0]
        unit_rr = [0]
        blk_rr = [0]

        def evac(out_ap, in_ap, rkeys, wkeys, scale=None):
            evac_rr[0] += 1
            if evac_rr[0] % 2 == 0:
                if scale is None:
                    S.add("scalar", lambda e: e.activation(out=out_ap, in_=in_ap, func=AF.Copy), reads=rkeys, writes=wkeys)
                else:
                    S.add("scalar", lambda e: e.activation(out=out_ap, in_=in_ap, func=AF.Copy, scale=scale),
                          reads=rkeys, writes=wkeys)
            else:
                if scale is None:
                    S.add("vector", lambda e: e.tensor_copy(out=out_ap, in_=in_ap), reads=rkeys, writes=wkeys)
                else:
                    S.add("vector", lambda e: e.tensor_scalar(out=out_ap, in0=in_ap, scalar1=scale, scalar2=None,
                                                              op0=ALU.mult), reads=rkeys, writes=wkeys)

        proj_rr = [0]

        def project(wch, wcol, dst, dst_off_fn, tiles, hoff, scale=None):
            for (t0, W) in tiles:
                bank = 2 + proj_rr[0] % 6
                proj_rr[0] += 1
                for kc in range(8):
                    S.add("tensor", (lambda e, kc=kc, t0=t0, W=W, bank=bank: e.matmul(
                        PS[bank][:, 0:W], lhsT=wch.ap[:, kc, wcol:wcol + 128], rhs=hT.ap[:, kc, hoff + t0:hoff + t0 + W],
                        start=(kc == 0), stop=(kc == 7))),
                        reads=wch.k2(kc, 0, 384) + hT.k2(kc, hoff + t0, hoff + t0 + W), writes=psk(bank))
                d0 = dst_off_fn(t0)
                evac(dst.ap[:, d0:d0 + W], PS[bank][:, 0:W], psk(bank), dst.k(d0, d0 + W), scale)

        def unit(KL, n, kap_fn, qap_fn, tab_ap, kmask_ap, val_fn, vslot, U_ap, D_ap, udbank, first, last, rk_extra,
                 after=()):
            u = unit_rr[0] % 3
            u4 = unit_rr[0] % 6
            unit_rr[0] += 1
            sb0, sb1 = 2 + 2 * u, 3 + 2 * u
            tt, pp = TT[u4], PP[u4]
            for hh in range(2):
                bank = sb0 + hh
                has_val = val_fn is not None
                S.add("tensor", (lambda e, hh=hh, bank=bank, has_val=has_val: e.matmul(
                    PS[bank][0:KL, 0:n], lhsT=kap_fn(hh), rhs=qap_fn(hh), start=True, stop=not has_val)),
                    reads=kT.k() + qT.k(), writes=psk(bank))
                if has_val:
                    S.add("tensor", (lambda e, hh=hh, bank=bank: e.matmul(
                        PS[bank][0:KL, 0:n], lhsT=INDB[64 * hh:64 * hh + 64, :], rhs=val_fn(hh), start=False, stop=True)),
                        reads=["INDB", "VALB"], writes=psk(bank))
            S.add("vector", lambda e: e.tensor_tensor(out=tt.ap[0:KL, :, 0:n], in0=PS_T[0:KL, sb0:sb0 + 2, 0:n],
                                                      in1=tab_ap, op=ALU.add),
                  reads=psk(sb0) + psk(sb1) + rk_extra, writes=tt.k())
            if kmask_ap is not None:
                S.add("scalar", lambda e: e.activation(out=pp.ap[0:KL, :, 0:n], in_=tt.ap[0:KL, :, 0:n], func=AF.Exp,
                                                       bias=kmask_ap), reads=tt.k() + ["KMA"], writes=pp.k())
            else:
                S.add("scalar", lambda e: e.activation(out=pp.ap[0:KL, :, 0:n], in_=tt.ap[0:KL, :, 0:n], func=AF.Exp),
                      reads=tt.k(), writes=pp.k())

            def back():
                seq = [(U_ap[0:64], VB.ap[0:KL, vslot, 0:64], 0), (U_ap[64:128], VB.ap[0:KL, vslot, 64:128], 1),
                       (D_ap[0:64], ONES[0:KL, 0:64], 0), (D_ap[64:128], ONES[0:KL, 0:64], 1)]
                for i, (o_ap, l_ap, hh) in enumerate(seq):
                    st_ = first and i < 2
                    sp_ = last and i == 3
                    S.add("tensor", (lambda e, o_ap=o_ap, l_ap=l_ap, hh=hh, st_=st_, sp_=sp_: e.matmul(
                        o_ap, lhsT=l_ap, rhs=pp.ap[0:KL, hh, 0:n], start=st_, stop=sp_, skip_group_check=True)),
                        reads=pp.k() + VB.k(128 * vslot, 128 * (vslot + 1)) + ["ONES"], writes=psk(udbank))
                for f in after:
                    f()
            pending.append(back)
            while len(pending) > DEPTH:
                pending.pop(0)()

        def vtrans(src_ap_fn, ntile, kl_fn):
            flush()
            for g0 in range(0, ntile, 4):
                bank = 2 + proj_rr[0] % 6
                proj_rr[0] += 1
                cnt = min(4, ntile - g0)
                for i in range(cnt):
                    KL = kl_fn(g0 + i)
                    S.add("tensor", (lambda e, i=i, g0=g0, KL=KL, bank=bank: e.matmul(
                        PS[bank][0:KL, 128 * i:128 * i + 128], lhsT=src_ap_fn(g0 + i, KL), rhs=IDENT[:],
                        start=True, stop=True)), reads=vT.k() + ["IDENT"], writes=psk(bank))
                src = PS[bank][:, 0:128 * cnt].rearrange("p (t f) -> p t f", t=cnt)
                evac(VB.ap[:, g0:g0 + cnt, :], src, psk(bank), VB.k(128 * g0, 128 * (g0 + cnt)))

        qtiles = [(512 * i, 512) for i in range(4)] + [(2048, 128)]
        ga_v = ga_d
        wdn_s = nc.dram_tensor("wdn_scratch", [8, 128, 22 * 128], BF16).ap()
        wdn_pre_v = wdn_d.rearrange("(f p) m -> p f m", p=128)

        def load_pair(p):
            S.add("gpsimd", lambda e: e.dma_start(out=wdn_s[p].rearrange("p (f m) -> p f m", f=22),
                                                  in_=wdn_pre_v[:, :, 128 * p:128 * p + 128]),
                  writes=[("wdn_s", p)], dma=True, semkey="wdpre%d" % (p % 2))
            wch, tab, j = WCH[p % 2], TAB[p % 2], p % 4
            cbase = 0 if p < 4 else 1536

            def wl(e):
                return [e.dma_start(out=wch.ap[:, :, 128 * i:128 * i + 128],
                                    in_=win_v[:, :, cbase + 512 * i + 128 * j:cbase + 512 * i + 128 * j + 128])
                        for i in range(3)]
            S.add("gpsimd", wl, writes=wch.k(), dma=True, semkey="wch%d" % (p % 2), ndma=3)
            if p < 4:
                tabA_ = tab.ap[:, 0:1536].rearrange("p (g h c) -> p g h c", g=3, h=2)
                S.add("sync", lambda e: e.dma_start(out=tabA_, in_=ga_v[:, :, 2 * j:2 * j + 2, :]),
                      writes=tab.k(), dma=True, semkey="tab%d" % (p % 2))
            else:
                tabB_ = tab.ap.rearrange("p (h c) -> p h c", h=2)
                S.add("sync", lambda e: e.dma_start(out=tabB_, in_=tb_d[:, 2 * j:2 * j + 2, :]),
                      writes=tab.k(), dma=True, semkey="tab%d" % (p % 2))

        xT_v = xT_d.rearrange("(c p) s -> p c s", p=128)
        XT = [AR(0, [128, 8, 512], F32), AR(16384, [128, 8, 512], F32), A2(20480, [128, 8, 512], F32)]
        SQH = [AR(56064, [128, 8, 512], BF16), AR(64256, [128, 8, 512], BF16)]
        RSH = [AR(72448, [128, 512], F32), AR(74496, [128, 512], F32), AR(76544, [128, 512], F32)]
        TMPH = [AR(78592, [128, 512], F32), AR(80640, [128, 512], F32)]
        load_pair(0)

        def h_front(ti):
            s0, W = slab_tiles[ti]
            xt, sqb, rs, bank = XT[ti % 3], SQH[ti % 2], RSH[ti % 3], ti % 2
            S.add("sync", lambda e: e.dma_start(out=xt.ap[:, :, 0:W], in_=xT_v[:, :, s0:s0 + W]),
                  writes=xt.k(), dma=True, semkey="xt%d" % (ti % 3))
            S.add("scalar", lambda e: e.activation(out=sqb.ap[:, 0:4, 0:W], in_=xt.ap[:, 0:4, 0:W], func=AF.Square),
                  reads=xt.k(0, 4 * 512), writes=sqb.k(0, 4 * 512))
            S.add("gpsimd", lambda e: e.tensor_tensor(out=sqb.ap[:, 4:8, 0:W], in0=xt.ap[:, 4:8, 0:W],
                                                      in1=xt.ap[:, 4:8, 0:W], op=ALU.mult),
                  reads=xt.k(4 * 512, 8 * 512), writes=sqb.k(4 * 512, 8 * 512))
            for c in range(8):
                S.add("tensor", (lambda e, c=c: e.matmul(PS[bank][:, 0:W], lhsT=ONES[:], rhs=sqb.ap[:, c, 0:W],
                                                         start=(c == 0), stop=(c == 7))),
                      reads=["ONES"] + sqb.k2(c, 0, 512), writes=psk(bank))

        def h_front_b(ti):
            s0, W = slab_tiles[ti]
            rs, bank = RSH[ti % 3], ti % 2
            S.add("scalar", lambda e: e.activation(out=rs.ap[:, 0:W], in_=PS[bank][:, 0:W], func=AF.Sqrt,
                                                   bias=1e-6, scale=1.0 / 1024.0),
                  reads=psk(bank), writes=rs.k())
            S.add("vector", lambda e: e.reciprocal(out=rs.ap[:, 0:W], in_=rs.ap[:, 0:W]), reads=rs.k(), writes=rs.k())

        def h_back(ti):
            s0, W = slab_tiles[ti]
            xt, rs = XT[ti % 3], RSH[ti % 3]
            for c in range(8):
                tm = TMPH[c % 2]
                S.add("vector", (lambda e, c=c, tm=tm: e.scalar_tensor_tensor(
                    out=tm.ap[:, 0:W], in0=xt.ap[:, c, 0:W], scalar=a1(c), in1=rs.ap[:, 0:W],
                    op0=ALU.mult, op1=ALU.mult)),
                    reads=xt.k2(c, 0, 512) + rs.k() + ["COL"], writes=tm.k())
                S.add("scalar", (lambda e, c=c, tm=tm: e.activation(
                    out=hT.ap[:, c, s0:s0 + W], in_=tm.ap[:, 0:W], func=AF.Identity, bias=b1(c))),
                    reads=tm.k() + ["ADAT"], writes=hT.k2(c, s0, s0 + W))

        def h_proj(ti):
            s0, W = slab_tiles[ti]
            project(WCH[0], 128, kT, lambda t0: t0, [(s0, W)], 0)
            project(WCH[0], 256, vT, lambda t0: t0, [(s0, W)], 0)

        h_front(0)
        h_front_b(0)
        for ti in range(len(slab_tiles)):
            if ti + 1 < len(slab_tiles):
                h_front(ti + 1)
            h_back(ti)
            if ti + 1 < len(slab_tiles):
                h_front_b(ti + 1)
            h_proj(ti)

        for j in range(4):
            wch = WCH[j % 2]
            tab = TAB[j % 2]
            tabA = tab.ap[:, 0:1536].rearrange("p (g h c) -> p g h c", g=3, h=2)
            ada_def_load(2 * j)
            if j > 0:
                project(wch, 128, kT, lambda t0: t0, slab_tiles, 0)
                project(wch, 256, vT, lambda t0: t0, slab_tiles, 0)
            project(wch, 0, qT, lambda t0: t0, qtiles, QOFF, scale=0.125)
            load_pair(j + 1)
            ada_def_mm(2 * j)
            ada_def_load(2 * j + 1)
            for g, (d, jq0, nqc, nt, blocks, tl) in enumerate(A_GEOM):
                if g == 1:
                    ada_def_mm(2 * j + 1)
                rgroups = [list(range(d))] if d * nt <= 24 else [list(range(0, 8)), list(range(8, 16))]
                for rg in rgroups:
                    slot_of = {}
                    lst = []
                    for r in rg:
                        for t in range(nt):
                            slot_of[(r, t)] = len(lst)
                            lst.append((r, t))

                    def vsrc(i, KL, lst=lst, d=d, jq0=jq0):
                        r, t = lst[i]
                        s0 = r + d * (jq0 - 64 + 128 * t)
                        return vT.ap[:, s0:s0 + d * (KL - 1) + 1:d]
                    vtrans(vsrc, len(lst), lambda i, lst=lst, tl=tl: tl[lst[i][1]])
                    for r in rg:
                        for (b, nqb) in blocks:
                            bu = blk_rr[0] % 2
                            blk_rr[0] += 1
                            tiles = [t for t in range(nt) if (128 * t - 64 + tl[t] + 64 > 256 * b) and
                                     (128 * t - 64 - 64 < 256 * b + nqb)]
                            c_lo = r + d * 256 * b
                            c_hi = c_lo + d * (nqb - 1) + 1
                            uv = UACC.ap[:, c_lo:c_hi:d]
                            dv = DACC.ap[:, c_lo:c_hi:d]

                            def evac_blk(g=g, uv=uv, dv=dv, bu=bu, nqb=nqb, c_lo=c_lo, c_hi=c_hi):
                                if g == 0:
                                    S.add("scalar", lambda e: e.activation(out=uv, in_=PS[bu][:, 0:nqb], func=AF.Copy),
                                          reads=psk(bu), writes=UACC.k(c_lo, c_hi))
                                    S.add("scalar", lambda e: e.activation(out=dv, in_=PS[bu][:, 256:256 + nqb], func=AF.Copy),
                                          reads=psk(bu), writes=DACC.k(c_lo, c_hi))
                                else:
                                    S.add("vector", lambda e: e.tensor_tensor(out=uv, in0=PS[bu][:, 0:nqb], in1=uv, op=ALU.add),
                                          reads=psk(bu) + UACC.k(c_lo, c_hi), writes=UACC.k(c_lo, c_hi))
                                    S.add("vector", lambda e: e.tensor_tensor(out=dv, in0=PS[bu][:, 256:256 + nqb], in1=dv,
                                                                              op=ALU.add),
                                          reads=psk(bu) + DACC.k(c_lo, c_hi), writes=DACC.k(c_lo, c_hi))
                            for ti_, t in enumerate(tiles):
                                KL = tl[t]
                                kA = 128 * t - 64
                                n0 = max(0, kA - 64 - 256 * b)
                                n1 = min(nqb, kA + KL + 64 - 256 * b)
                                n = n1 - n0
                                c0 = 256 * b + n0 - kA + 64
                                ks0 = r + d * (jq0 + kA)
                                qs0 = r + d * (256 * b + n0)
                                kap = lambda hh, ks0=ks0, KL=KL, d=d: kT.ap[64 * hh:64 * hh + 64, ks0:ks0 + d * (KL - 1) + 1:d]
                                qap = lambda hh, qs0=qs0, n=n, d=d: qT.ap[64 * hh:64 * hh + 64, qs0:qs0 + d * (n - 1) + 1:d]
                                tab_ap = tabA[0:KL, g, :, c0:c0 + n]
                                kidx = A_TIDX[(g, r, t)]
                                lastt = ti_ == len(tiles) - 1
                                unit(KL, n, kap, qap, tab_ap, KMA[0:KL, kidx:kidx + 1], None, slot_of[(r, t)],
                                     PS[bu][:, n0:n1], PS[bu][:, 256 + n0:256 + n1], bu,
                                     ti_ == 0, lastt, tab.k(), after=([evac_blk] if lastt else ()))
            flush()
            S.add("vector", lambda e: e.tensor_scalar(out=DACC.ap, in0=DACC.ap, scalar1=1e-30, scalar2=None, op0=ALU.add),
                  reads=DACC.k(), writes=DACC.k())
            S.add("vector", lambda e: e.reciprocal(out=DACC.ap, in_=DACC.ap), reads=DACC.k(), writes=DACC.k())
            S.add("gpsimd", (lambda e, j=j: e.tensor_tensor(out=OAB.ap[:, j, :], in0=UACC.ap, in1=DACC.ap, op=ALU.mult)),
                  reads=UACC.k() + DACC.k(), writes=OAB.k2(j, 0, NQ))

        WM = [A2(0, [128, 24, 128], BF16), A2(6144, [128, 24, 128], BF16)]
        wbd_v = wbd_d.rearrange("(c p) f -> p c f", p=128)
        wbn_v = wbn_d.rearrange("(c p) f -> p c f", p=128)

        def wloadm_j(j):
            wm = WM[j % 2]

            def wl(e):
                return [e.dma_start(out=wm.ap[:, 0:8, :], in_=win_v[:, :, 3072 + 128 * j:3072 + 128 * j + 128]),
                        e.dma_start(out=wm.ap[:, 8:16, :], in_=win_v[:, :, 4096 + 128 * j:4096 + 128 * j + 128]),
                        e.dma_start(out=wm.ap[:, 16:20, :], in_=wbd_v[:, :, 128 * j:128 * j + 128]),
                        e.dma_start(out=wm.ap[:, 20:24, :], in_=wbn_v[:, :, 128 * j:128 * j + 128])]
            S.add("gpsimd", wl, writes=wm.k(), dma=True, semkey="wm%d" % (j % 2), ndma=4)

        bk_tiles = [(512 * i, 512) for i in range(5)] + [(2560, 256)]
        for j in range(4):
            wch = WCH[j % 2]
            tab = TAB[j % 2]

            tabB = tab.ap.rearrange("p (h c) -> p h c", h=2)
            project(wch, 128, kT, lambda t0: t0, bk_tiles, BKOFF)
            project(wch, 256, vT, lambda t0: t0, bk_tiles, BKOFF)
            project(wch, 0, qT, lambda t0: t0, qtiles, QOFF, scale=0.125)
            if j < 3:
                load_pair(4 + j + 1)
            else:
                wloadm_j(0)
            vtrans(lambda i, KL: vT.ap[:, 128 * i:128 * i + 128], 21, lambda i: 128)
            for (G, nr, tiles) in B_GROUPS:
                bu = blk_rr[0] % 2
                blk_rr[0] += 1
                n = 64 * nr
                q0 = 256 * G

                def evac_b(bu=bu, q0=q0, n=n):
                    S.add("scalar", lambda e: e.activation(out=UACC.ap[:, q0:q0 + n], in_=PS[bu][:, 0:n], func=AF.Copy),
                          reads=psk(bu), writes=UACC.k(q0, q0 + n))
                    S.add("scalar", lambda e: e.activation(out=DACC.ap[:, q0:q0 + n], in_=PS[bu][:, 256:256 + n],
                                                           func=AF.Identity, bias=1e-30),
                          reads=psk(bu), writes=DACC.k(q0, q0 + n))
                for ti_, lm in enumerate(tiles):
                    blk0 = 12 - 2 * lm + 4 * G
                    kap = lambda hh, lm=lm: kT.ap[64 * hh:64 * hh + 64, 128 * lm:128 * lm + 128]
                    qap = lambda hh, q0=q0, n=n: qT.ap[64 * hh:64 * hh + 64, q0:q0 + n]
                    tab_ap = tabB[:, :, 64 * blk0:64 * blk0 + n]
                    vi = B_UIDX[(G, lm)]
                    valf = lambda hh, vi=vi, nr=nr: VALB[64 * hh:64 * hh + 64, vi:vi + nr].unsqueeze(2).to_broadcast([64, nr, 64])
                    lastt = ti_ == len(tiles) - 1
                    unit(128, n, kap, qap, tab_ap, None, valf, lm, PS[bu][:, 0:n], PS[bu][:, 256:256 + n], bu,
                         ti_ == 0, lastt, tab.k(), after=([evac_b] if lastt else ()))
            flush()
            S.add("vector", lambda e: e.reciprocal(out=DACC.ap, in_=DACC.ap), reads=DACC.k(), writes=DACC.k())
            S.add("gpsimd", (lambda e, j=j: e.tensor_tensor(out=OAB.ap[:, 4 + j, :], in0=UACC.ap, in1=DACC.ap, op=ALU.mult)),
                  reads=UACC.k() + DACC.k(), writes=OAB.k2(4 + j, 0, NQ))

        wup_v = wup_d.rearrange("(c p) f -> p c f", p=128)
        def wup_load(pos):
            fj, half = pos // 2, pos % 2
            cc = fj + 22 * half
            S.add("gpsimd", lambda e: e.dma_start(out=WUP.ap[:, pos, :, :], in_=wup_v[:, :, 128 * cc:128 * cc + 128]),
                  writes=WUP.k(1024 * pos, 1024 * (pos + 1)), dma=True, semkey="wup%d" % (pos % 4))
        ETMP = [[A2(12288 + 8192 * s_ + 2048 * i, [128, 512], F32) for i in range(4)] for s_ in range(2)]
        WOUT = A2(28672, [128, 8, 1024], BF16)
        wout_v = wout_d.rearrange("(c p) f -> p c f", p=128)
        m1_rr = 0
        for j in range(8):
            wm = WM[j % 2]
            if j + 1 < 8:
                wloadm_j(j + 1)
            if j == 1:
                for hf in range(2):
                    S.add("gpsimd", (lambda e, hf=hf: e.dma_start(out=WOUT.ap[:, :, 512 * hf:512 * hf + 512],
                                                                  in_=wout_v[:, :, 512 * hf:512 * hf + 512])),
                          writes=WOUT.k(), dma=True, semkey="wout")
            if j == 6:
                for pos in range(34, 44):
                    wup_load(pos)
            for (t0, W) in qtiles:
                s_ = m1_rr % 2
                m1_rr += 1
                banks = [4 * s_ + i for i in range(4)]
                et = ETMP[s_]
                for kc in range(8):
                    S.add("tensor", (lambda e, kc=kc, t0=t0, W=W, b=banks[0], wm=wm: e.matmul(
                        PS[b][:, 0:W], lhsT=wm.ap[:, kc, :], rhs=hT.ap[:, kc, QOFF + t0:QOFF + t0 + W],
                        start=(kc == 0), stop=(kc == 7))),
                        reads=wm.k() + hT.k2(kc, QOFF + t0, QOFF + t0 + W), writes=psk(banks[0]))
                for kc in range(8):
                    S.add("tensor", (lambda e, kc=kc, t0=t0, W=W, b=banks[1], wm=wm: e.matmul(
                        PS[b][:, 0:W], lhsT=wm.ap[:, 8 + kc, :], rhs=hT.ap[:, kc, QOFF + t0:QOFF + t0 + W],
                        start=(kc == 0), stop=(kc == 7))),
                        reads=wm.k() + hT.k2(kc, QOFF + t0, QOFF + t0 + W), writes=psk(banks[1]))
                for kc in range(4):
                    S.add("tensor", (lambda e, kc=kc, t0=t0, W=W, b=banks[2], wm=wm: e.matmul(
                        PS[b][:, 0:W], lhsT=wm.ap[:, 16 + kc, :], rhs=OAB.ap[:, kc, t0:t0 + W],
                        start=(kc == 0), stop=(kc == 3))),
                        reads=wm.k() + OAB.k2(kc, t0, t0 + W), writes=psk(banks[2]))
                for kc in range(4):
                    S.add("tensor", (lambda e, kc=kc, t0=t0, W=W, b=banks[3], wm=wm: e.matmul(
                        PS[b][:, 0:W], lhsT=wm.ap[:, 20 + kc, :], rhs=OAB.ap[:, 4 + kc, t0:t0 + W],
                        start=(kc == 0), stop=(kc == 3))),
                        reads=wm.k() + OAB.k2(4 + kc, t0, t0 + W), writes=psk(banks[3]))
                S.add("scalar", (lambda e, W=W, b=banks[0], et=et: e.activation(out=et[0].ap[:, 0:W], in_=PS[b][:, 0:W],
                                                                              func=AF.Sigmoid)),
                      reads=psk(banks[0]), writes=et[0].k())
                S.add("scalar", (lambda e, W=W, b=banks[1], et=et: e.activation(out=et[1].ap[:, 0:W], in_=PS[b][:, 0:W],
                                                                              func=AF.Sigmoid)),
                      reads=psk(banks[1]), writes=et[1].k())
                S.add("vector", (lambda e, W=W, b=banks[2], et=et: e.tensor_tensor(
                    out=et[2].ap[:, 0:W], in0=PS[b][:, 0:W], in1=et[0].ap[:, 0:W], op=ALU.mult)),
                    reads=psk(banks[2]) + et[0].k(), writes=et[2].k())
                S.add("vector", (lambda e, W=W, b=banks[3], et=et: e.tensor_tensor(
                    out=et[3].ap[:, 0:W], in0=PS[b][:, 0:W], in1=et[1].ap[:, 0:W], op=ALU.mult)),
                    reads=psk(banks[3]) + et[1].k(), writes=et[3].k())
                S.add("gpsimd", (lambda e, W=W, et=et, j=j, t0=t0: e.tensor_tensor(
                    out=MERG.ap[:, j, t0:t0 + W], in0=et[2].ap[:, 0:W], in1=et[3].ap[:, 0:W], op=ALU.add)),
                    reads=et[2].k() + et[3].k(), writes=MERG.k2(j, t0, t0 + W))

        for pos in range(17):
            wup_load(pos)
        XT2 = A2(12288, [128, 8, 512], F32)
        m2_rr = 0
        for (t0, W) in qtiles:
            S.add("sync", (lambda e, t0=t0, W=W: e.dma_start(out=XT2.ap[:, :, 0:W], in_=xT_v[:, :, QOFF + t0:QOFF + t0 + W])),
                  writes=XT2.k(), dma=True, semkey="xt0")
            for j in range(8):
                bank = m2_rr % 8
                m2_rr += 1
                for kc in range(8):
                    S.add("tensor", (lambda e, kc=kc, t0=t0, W=W, bank=bank, j=j: e.matmul(
                        PS[bank][:, 0:W], lhsT=WOUT.ap[:, kc, 128 * j:128 * j + 128], rhs=MERG.ap[:, kc, t0:t0 + W],
                        start=(kc == 0), stop=(kc == 7))),
                        reads=WOUT.k() + MERG.k2(kc, t0, t0 + W), writes=psk(bank))
                S.add("vector", (lambda e, t0=t0, W=W, bank=bank, j=j: e.scalar_tensor_tensor(
                    out=xmid.ap[:, j, t0:t0 + W], in0=PS[bank][:, 0:W], scalar=gate1(j), in1=XT2.ap[:, j, 0:W],
                    op0=ALU.mult, op1=ALU.add)),
                    reads=psk(bank) + XT2.k2(j, 0, 512) + ["ADATb"], writes=xmid.k2(j, t0, t0 + W))

        wup_v = wup_d.rearrange("(c p) f -> p c f", p=128)
        wdn_v = wdn_d.rearrange("(f p) m -> p f m", p=128)
        yT_v = yT_d.rearrange("(c p) t -> p c t", p=128)
        WDN = [A2(0, [128, 22, 128], BF16), A2(5632, [128, 22, 128], BF16)]
        GTB = A2(11264, [128, 22, FW], BF16)
        H2 = A2(26624, [128, 8, FW], BF16)
        HB = A2(26624 + 5504, [128, 8, 2], BF16)
        TV = [A2(32256 + 1536 * i, [128, FW], F32) for i in range(3)]
        TG = [A2(32256 + 1536 * (3 + i), [128, FW], F32) for i in range(3)]
        GG = [A2(32256 + 1536 * (6 + i), [128, FW], F32) for i in range(2)]
        RS2 = A2(44544, [128, FW], F32)
        TMPF = A2(46080, [128, FW], F32)
        SQC = [A2(46080, [128, FW], BF16), A2(46080 + 768, [128, FW], BF16)]
        YB = [A2(47616, [128, FW], F32), A2(47616, [128, FW], F32)]

        for pos in range(17, 34):
            wup_load(pos)
        zt = []
        zc = 63
        while zc + 1 < 2112:
            W = min(FW, 2113 - zc)
            zt.append((zc, W))
            zc += W - 2
        wdn_rr = [0]
        ffn_rr = 0
        y_rr = 0

        cur_tile = [1]

        def wdn_load(j):
            b = wdn_rr[0] % 2
            wd = WDN[b]
            wdf = A2(5632 * b, [128, 22 * 128], BF16)
            if cur_tile[0] == 0:
                S.add("gpsimd", (lambda e, wd=wd, j=j: e.dma_start(out=wd.ap, in_=wdn_v[:, :, 128 * j:128 * j + 128])),
                      writes=wd.k(), dma=True, semkey="wdn%d" % b)
                S.add("sync", (lambda e, wdf=wdf, j=j: e.dma_start(out=wdn_s[j], in_=wdf.ap)),
                      reads=wd.k(), writes=[("wdn_s", j)], dma=True, semkey="wds%d" % b)
            else:
                S.add("sync", (lambda e, wdf=wdf, j=j: e.dma_start(out=wdf.ap, in_=wdn_s[j])),
                      reads=[("wdn_s", j)], writes=wd.k(), dma=True, semkey="wdq%d" % b)
            wdn_rr[0] += 1
            return wd

        def h2_tile(ti, part=0):
            zc0, W = zt[ti]
            cs = 0 if ti == 0 else 2
            if part in (0, 1):
                h2_part1(ti, zc0, W, cs)
            if part in (0, 2):
                h2_part2(ti, zc0, W, cs)

        def h2_part1(ti, zc0, W, cs):
            for c in range(8):
                sq = SQC[c % 2]
                S.add("scalar", (lambda e, c=c, sq=sq: e.activation(
                    out=sq.ap[:, cs:W], in_=xmid.ap[:, c, zc0 + cs:zc0 + W], func=AF.Square)),
                    reads=xmid.k2(c, zc0 + cs, zc0 + W), writes=sq.k())
                S.add("tensor", (lambda e, c=c, sq=sq: e.matmul(PS[0][:, cs:W], lhsT=ONES[:], rhs=sq.ap[:, cs:W],
                                                              start=(c == 0), stop=(c == 7))),
                      reads=["ONES"] + sq.k(), writes=psk(0))
            S.add("scalar", lambda e: e.activation(out=RS2.ap[:, cs:W], in_=PS[0][:, cs:W], func=AF.Sqrt,
                                                   bias=1e-6, scale=1.0 / 1024.0),
                  reads=psk(0), writes=RS2.k())
            S.add("vector", lambda e: e.reciprocal(out=RS2.ap[:, cs:W], in_=RS2.ap[:, cs:W]),
                  reads=RS2.k(), writes=RS2.k())

        def h2_part2(ti, zc0, W, cs):
            if ti > 0:
                S.add("gpsimd", lambda e: e.tensor_copy(out=H2.ap[:, :, 0:2], in_=HB.ap), reads=HB.k(), writes=H2.k())
            for c in range(8):
                S.add("vector", (lambda e, c=c: e.scalar_tensor_tensor(
                    out=TMPF.ap[:, cs:W], in0=xmid.ap[:, c, zc0 + cs:zc0 + W], scalar=a2(c), in1=RS2.ap[:, cs:W],
                    op0=ALU.mult, op1=ALU.mult)),
                    reads=xmid.k2(c, zc0 + cs, zc0 + W) + RS2.k() + ["COL2"], writes=TMPF.k())
                S.add("scalar", (lambda e, c=c: e.activation(
                    out=H2.ap[:, c, cs:W], in_=TMPF.ap[:, cs:W], func=AF.Identity, bias=b2(c))),
                    reads=TMPF.k() + ["ADATb"], writes=H2.k2(c, 0, FW))
            if ti == 0:
                S.add("gpsimd", lambda e: e.tensor_scalar(out=H2.ap[:, :, 0:1], in0=H2.ap[:, :, 0:1], scalar1=ZM[:, 0:1],
                                                          scalar2=None, op0=ALU.mult), reads=H2.k() + ["ZM"], writes=H2.k())
            if ti == len(zt) - 1:
                S.add("gpsimd", lambda e: e.tensor_scalar(out=H2.ap[:, :, W - 1:W], in0=H2.ap[:, :, W - 1:W],
                                                          scalar1=ZM[:, 1:2], scalar2=None, op0=ALU.mult),
                      reads=H2.k() + ["ZM"], writes=H2.k())
            else:
                S.add("gpsimd", lambda e: e.tensor_copy(out=HB.ap, in_=H2.ap[:, :, W - 2:W]),
                      reads=H2.k(), writes=HB.k())

        h2_tile(0)
        for ti, (zc0, W) in enumerate(zt):
            nout = W - 2
            wds = [wdn_load(0), wdn_load(1)]
            stage_b = []
            fj_order = list(range(22)) if ti > 0 else list(range(0, 8)) + list(range(17, 22)) + list(range(8, 17))
            for fj in fj_order:
                u = ffn_rr % 2
                u3 = ffn_rr % 3
                ffn_rr += 1
                bv, bg = 2 * u3, 2 * u3 + 1
                for half, bank in ((0, bv), (1, bg)):
                    pos = 2 * fj + half
                    for kc in range(8):
                        S.add("tensor", (lambda e, kc=kc, pos=pos, bank=bank, W=W: e.matmul(
                            PS[bank][:, 0:W], lhsT=WUP.ap[:, pos, kc, :], rhs=H2.ap[:, kc, 0:W],
                            start=(kc == 0), stop=(kc == 7))),
                            reads=WUP.k(1024 * pos + 128 * kc, 1024 * pos + 128 * kc + 128) + H2.k2(kc, 0, FW),
                            writes=psk(bank))
                gg, tv, tg = GG[u], TV[u3], TG[u3]
                cv, cg = fj, fj + 22
                if stage_b:
                    stage_b[0][0]()
                S.add("scalar", (lambda e, cc=cv, bank=bv, tb=tv, nout=nout: e.activation(
                    out=tb.ap[:, 0:nout], in_=PS[bank][:, 1:1 + nout], func=AF.Identity,
                    bias=CONV[:, 3, cc:cc + 1], scale=CONV[:, 1, cc:cc + 1])),
                    reads=psk(bv) + ["CONV"], writes=tv.k())
                S.add("scalar", (lambda e, cc=cg, bank=bg, tb=tg, nout=nout: e.activation(
                    out=tb.ap[:, 0:nout], in_=PS[bank][:, 1:1 + nout], func=AF.Identity,
                    bias=CONV[:, 3, cc:cc + 1], scale=CONV[:, 1, cc:cc + 1])),
                    reads=psk(bg) + ["CONV"], writes=tg.k())
                S.add("scalar", (lambda e, cc=cg, bank=bg, gg=gg, nout=nout: e.activation(
                    out=gg.ap[:, 0:nout], in_=PS[bank][:, 0:nout], func=AF.Identity, scale=CONV[:, 0, cc:cc + 1])),
                    reads=psk(bg) + ["CONV"], writes=gg.k())
                S.add("vector", (lambda e, cc=cv, bank=bv, tb=tv, nout=nout: e.scalar_tensor_tensor(
                    out=tb.ap[:, 0:nout], in0=PS[bank][:, 0:nout], scalar=CONV[:, 0, cc:cc + 1], in1=tb.ap[:, 0:nout],
                    op0=ALU.mult, op1=ALU.add)), reads=psk(bv) + ["CONV"] + tv.k(), writes=tv.k())
                S.add("vector", (lambda e, cc=cv, bank=bv, tb=tv, nout=nout: e.scalar_tensor_tensor(
                    out=tb.ap[:, 0:nout], in0=PS[bank][:, 2:2 + nout], scalar=CONV[:, 2, cc:cc + 1], in1=tb.ap[:, 0:nout],
                    op0=ALU.mult, op1=ALU.add)), reads=psk(bv) + ["CONV"] + tv.k(), writes=tv.k())
                S.add("gpsimd", (lambda e, tb=tg, gg=gg, nout=nout: e.tensor_tensor(
                    out=tb.ap[:, 0:nout], in0=tb.ap[:, 0:nout], in1=gg.ap[:, 0:nout], op=ALU.add)),
                    reads=tg.k() + gg.k(), writes=tg.k())

                def sb_dve(cg=cg, bg=bg, tg=tg, nout=nout):
                    S.add("vector", lambda e: e.scalar_tensor_tensor(
                        out=tg.ap[:, 0:nout], in0=PS[bg][:, 2:2 + nout], scalar=CONV[:, 2, cg:cg + 1], in1=tg.ap[:, 0:nout],
                        op0=ALU.mult, op1=ALU.add), reads=psk(bg) + ["CONV"] + tg.k(), writes=tg.k())

                def sb(fj=fj, tv=tv, tg=tg, nout=nout):
                    S.add("scalar", lambda e: e.activation(out=tg.ap[:, 0:nout], in_=tg.ap[:, 0:nout],
                                                           func=AF.Gelu_apprx_tanh), reads=tg.k(), writes=tg.k())
                    S.add("gpsimd", lambda e: e.tensor_tensor(out=GTB.ap[:, fj, 0:nout], in0=tg.ap[:, 0:nout],
                                                              in1=tv.ap[:, 0:nout], op=ALU.mult),
                          reads=tg.k() + tv.k(), writes=GTB.k2(fj, 0, FW))
                if stage_b:
                    stage_b.pop(0)[1]()
                stage_b.append((sb_dve, sb))
            while stage_b:
                d_, r_ = stage_b.pop(0)
                d_()
                r_()
            oc0 = zc0 + 1
            for j in range(8):
                wd = wds[j]
                bank = 6 + (j % 2)
                for fj in range(22):
                    S.add("tensor", (lambda e, fj=fj, wd=wd, bank=bank, nout=nout: e.matmul(
                        PS[bank][:, 0:nout], lhsT=wd.ap[:, fj, :], rhs=GTB.ap[:, fj, 0:nout],
                        start=(fj == 0), stop=(fj == 21))),
                        reads=wd.k() + GTB.k2(fj, 0, FW), writes=psk(bank))
                if j + 2 < 8:
                    wds.append(wdn_load(j + 2))
                S.add("vector", (lambda e, j=j, bank=bank, nout=nout, oc0=oc0: e.scalar_tensor_tensor(
                    out=xmid.ap[:, j, oc0:oc0 + nout], in0=PS[bank][:, 0:nout], scalar=gate2(j),
                    in1=xmid.ap[:, j, oc0:oc0 + nout], op0=ALU.mult, op1=ALU.add)),
                    reads=psk(bank) + xmid.k2(j, oc0, oc0 + nout) + ["ADATb"], writes=xmid.k2(j, oc0, oc0 + nout))
                if ti + 1 < len(zt) and j == 1:
                    h2_tile(ti + 1, part=1)
                if ti + 1 < len(zt) and j == 3:
                    h2_tile(ti + 1, part=2)
            for c in range(8):
                sq = SQC[c % 2]
                S.add("scalar", (lambda e, c=c, sq=sq, oc0=oc0, nout=nout: e.activation(
                    out=sq.ap[:, 0:nout], in_=xmid.ap[:, c, oc0:oc0 + nout], func=AF.Square)),
                    reads=xmid.k2(c, oc0, oc0 + nout), writes=sq.k())
                S.add("tensor", (lambda e, c=c, sq=sq, nout=nout: e.matmul(PS[6][:, 0:nout], lhsT=ONES[:], rhs=sq.ap[:, 0:nout],
                                                                            start=(c == 0), stop=(c == 7))),
                      reads=["ONES"] + sq.k(), writes=psk(6))
            S.add("scalar", (lambda e, nout=nout: e.activation(out=RS2.ap[:, 0:nout], in_=PS[6][:, 0:nout], func=AF.Sqrt,
                                                               bias=1e-6, scale=1.0 / 1024.0)),
                  reads=psk(6), writes=RS2.k())
            S.add("vector", (lambda e, nout=nout: e.reciprocal(out=RS2.ap[:, 0:nout], in_=RS2.ap[:, 0:nout])),
                  reads=RS2.k(), writes=RS2.k())
            tk0 = oc0 - 64
            for c in range(8):
                yb = (TV + TG + GG)[c]
                ysem = "yo%d" % (y_rr % 4)
                y_rr += 1
                S.add("vector", (lambda e, c=c, yb=yb, oc0=oc0, nout=nout: e.scalar_tensor_tensor(
                    out=yb.ap[:, 0:nout], in0=xmid.ap[:, c, oc0:oc0 + nout], scalar=gfin(c), in1=RS2.ap[:, 0:nout],
                    op0=ALU.mult, op1=ALU.mult)),
                    reads=xmid.k2(c, oc0, oc0 + nout) + RS2.k() + ["GTC"], writes=yb.k())
                S.add("sync", (lambda e, c=c, yb=yb, tk0=tk0, nout=nout: e.dma_start(
                    out=yT_v[:, c, tk0:tk0 + nout], in_=yb.ap[:, 0:nout])),
                    reads=yb.k(), dma=True, semkey=ysem)

        S.emit(nc, st)
    return nc


_WUP_ALIAS_FIX = True


def _host_tables(rel_bias, na_rpb):
    p = np.arange(128)[:, None]
    c = np.arange(256)[None, :]
    delta = p - c + 64
    band = np.abs(delta) <= 64
    GA = np.full((128, 3, 8, 256), NEG, dtype=np.float32)
    for g, d in enumerate(DIL):
        bk = _t5_bucket(delta * d)
        vals = rel_bias[bk]
        vals = np.where(band[:, :, None], vals, np.float32(NEG))
        GA[:, g, :, :] = np.transpose(vals, (0, 2, 1))
    TB = np.zeros((128, 8, 1024), dtype=np.float32)
    kc = np.arange(64)[:, None]
    qc = np.arange(64)[None, :]
    cstart = np.clip(qc - 8, 0, 48)
    colok = (kc >= cstart) & (kc < cstart + 16)
    dc = np.clip(kc - qc, -15, 15) + 15
    for blk in range(16):
        e = 7 - blk
        for half in range(2):
            dr = e + half
            if 0 <= dr + 7 <= 14:
                v = na_rpb[:, dr + 7, :][:, dc]
            else:
                v = np.zeros((8, 64, 64), dtype=np.float32)
            v = np.where(colok[None], v, np.float32(NEG))
            TB[64 * half:64 * half + 64, :, 64 * blk:64 * blk + 64] = np.transpose(v, (1, 0, 2))
    return GA, TB


def _core_masks(cidx):
    T0 = TOK * cidx
    R0 = 32 * cidx
    km = np.zeros((128, NKT), dtype=np.float32)
    pp = np.arange(128)
    for g, (d, jq0, nqc, nt, blocks, tl) in enumerate(A_GEOM):
        for r in range(d):
            for t in range(nt):
                kk = jq0 - 64 + 128 * t + pp
                glob = T0 - SOFF + r + d * kk
                ok = (glob >= 0) & (glob < S_TOT)
                km[:, A_TIDX[(g, r, t)]] = np.where(ok, 0.0, NEG)
    val = np.zeros((128, NVB), dtype=np.float32)
    for G, nr, tiles in B_GROUPS:
        for lm in tiles:
            vi = B_UIDX[(G, lm)]
            for jj in range(nr):
                rr = R0 - 1 + 4 * G + jj
                for half in range(2):
                    kr = R0 - 6 + 2 * lm + half
                    if 0 <= rr < 256:
                        rs = min(max(rr - 4, 0), 248)
                        ok = (rs <= kr < rs + 8)
                    else:
                        ok = (0 <= kr < 256) and (rr - 4 <= kr < rr + 4)
                    v = 0.0 if ok else NEG
                    val[half, vi + jj] = v
                    val[64 + half, vi + jj] = v
    zm = np.ones((128, 2), dtype=np.float32)
    if cidx == 0:
        zm[:, 0] = 0.0
    if cidx == NCORE - 1:
        zm[:, 1] = 0.0
    return km, val.astype(ml_dtypes.bfloat16), zm


def kernel(x, c, w_ada, b_ada, g_mix, w_in, rel_bias, na_rpb, w_branch_dil, w_branch_na,
           w_out, g_ffn, w_up, conv_w, conv_b, w_down, g_final):
    f32 = np.float32
    x = np.asarray(x, dtype=f32)
    xT = np.ascontiguousarray(x[0].T)
    xTp = np.zeros((1024, S_TOT + 2 * SOFF), dtype=f32)
    xTp[:, SOFF:SOFF + S_TOT] = xT
    colmaj = lambda v, n: np.ascontiguousarray(np.asarray(v, dtype=f32).reshape(n, 128).T)
    cT = colmaj(c[0], 8)
    badaT = colmaj(b_ada[0], 48)
    gT = np.concatenate([colmaj(g_mix[0], 8), colmaj(g_ffn[0], 8), colmaj(g_final, 8)], axis=1)
    convT = np.zeros((128, 4, 44), dtype=f32)
    for k in range(3):
        convT[:, k, :] = colmaj(conv_w[0, k], 44)
    convT[:, 3, :] = colmaj(conv_b[0], 44)
    GA, TB = _host_tables(np.asarray(rel_bias, dtype=f32), np.asarray(na_rpb[0], dtype=f32))
    indB = np.zeros((128, 128), dtype=f32)
    for base in (0, 64):
        indB[base, 0:64] = 1.0
        indB[base + 1, 64:128] = 1.0
    indB = indB.astype(ml_dtypes.bfloat16)
    ident = np.eye(128, dtype=f32).astype(ml_dtypes.bfloat16)
    shared = {
        "cT": cT, "wada": np.ascontiguousarray(w_ada[0], dtype=f32), "badaT": badaT, "gT": gT,
        "w_in": np.ascontiguousarray(w_in[0], dtype=f32), "w_bd": np.ascontiguousarray(w_branch_dil[0], dtype=f32),
        "w_bn": np.ascontiguousarray(w_branch_na[0], dtype=f32), "w_out": np.ascontiguousarray(w_out[0], dtype=f32),
        "w_up": np.ascontiguousarray(w_up[0], dtype=f32), "convT": convT,
        "w_down": np.ascontiguousarray(w_down[0], dtype=f32), "GA": GA, "TBt": TB, "indB": indB, "ident": ident,
    }
    in_maps = []
    for ci in range(NCORE):
        km, val, zm = _core_masks(ci)
        m = dict(shared)
        m["xT"] = np.ascontiguousarray(xTp[:, TOK * ci:TOK * ci + SL])
        m["kmaskA"] = km
        m["valB"] = val
        m["zmask"] = zm
        in_maps.append(m)
    nc = build_program()
    res = run_bass_kernel_spmd(nc, in_maps, core_ids=list(range(NCORE)))
    yT = np.concatenate([np.asarray(r["yT"], dtype=f32) for r in res.results], axis=1)
    return np.ascontiguousarray(yT.T)[None].astype(f32)
```
